# Optimizing a Trainium2 kernel written in Bass

```python
import jax, jax.numpy as jnp
from jax import lax
import numpy as np

D_MODEL = 1024
BATCH = 4
SEQ = 4096
DEPTH = 1

MEM_LEN = 256
RET_WIDTH = D_MODEL // 2
RET_HEADS = 4
RET_DK = RET_WIDTH // RET_HEADS
RET_DV = RET_WIDTH // RET_HEADS
SB_WIDTH = D_MODEL - RET_WIDTH
SB_HEADS = 8
SB_DH = SB_WIDTH // SB_HEADS
MIX_WIDTH = RET_WIDTH + SB_WIDTH
IN_COLS = 4 * RET_WIDTH + 3 * SB_WIDTH
X_HEADS = 4
X_DH = D_MODEL // X_HEADS
D_FF = 4 * D_MODEL
CHUNK = 128
Q_BLOCK = 128
ROPE_BASE = 10000.0
EPS = 1e-6

kernel_name = "hybrid_retention_stickbreaking_block"


def rms_norm(t, gain):
    tf = t.astype(jnp.float32)
    return tf * lax.rsqrt(jnp.mean(tf * tf, axis=-1, keepdims=True) + EPS) * gain.astype(jnp.float32)


def split_heads(t, n_heads):
    b, s, _ = t.shape
    return t.reshape(b, s, n_heads, -1).transpose(0, 2, 1, 3)


def merge_heads(t):
    b, h, s, d = t.shape
    return t.transpose(0, 2, 1, 3).reshape(b, s, h * d)


def rotary(t, positions):
    d = t.shape[-1]
    inv_freq = ROPE_BASE ** (-jnp.arange(0, d, 2, dtype=jnp.float32) / d)
    ang = positions.astype(jnp.float32)[:, None, :, None] * inv_freq
    cos, sin = jnp.cos(ang), jnp.sin(ang)
    tf = t.astype(jnp.float32)
    t1, t2 = tf[..., : d // 2], tf[..., d // 2:]
    return jnp.concatenate([t1 * cos - t2 * sin, t2 * cos + t1 * sin], axis=-1)


def chunkwise_retention(q, k, v):
    b, h, s, dk = q.shape
    dv = v.shape[-1]
    n_chunks = s // CHUNK
    log_gamma = jnp.log1p(-(2.0 ** (-5.0 - jnp.arange(h, dtype=jnp.float32))))
    idx = jnp.arange(CHUNK, dtype=jnp.float32)
    rel = idx[:, None] - idx[None, :]
    causal = rel >= 0
    decay_in = jnp.where(causal, jnp.exp(log_gamma[:, None, None] * jnp.where(causal, rel, 0.0)), 0.0)
    q_decay = jnp.exp(log_gamma[:, None] * (idx + 1.0))
    k_decay = jnp.exp(log_gamma[:, None] * (CHUNK - 1.0 - idx))
    chunk_decay = jnp.exp(log_gamma * CHUNK)

    def to_chunks(t):
        return t.astype(jnp.float32).reshape(b, h, n_chunks, CHUNK, -1).transpose(2, 0, 1, 3, 4)

    qc, kc, vc = to_chunks(q), to_chunks(k), to_chunks(v)

    def step(state, inp):
        qi, ki, vi = inp
        scores = jnp.einsum('bhqd,bhkd->bhqk', qi, ki) * decay_in
        out = (jnp.einsum('bhqk,bhkv->bhqv', scores, vi)
               + jnp.einsum('bhqd,bhdv->bhqv', qi * q_decay[None, :, :, None], state))
        state = (state * chunk_decay[None, :, None, None]
                 + jnp.einsum('bhkd,bhkv->bhdv', ki * k_decay[None, :, :, None], vi))
        return state, out

    state0 = jnp.zeros((b, h, dk, dv), jnp.float32)
    _, out = lax.scan(step, state0, (qc, kc, vc))
    return out.transpose(1, 2, 0, 3, 4).reshape(b, h, s, dv)


def stick_breaking_attention(q, k, v):
    b, h, s, d = q.shape
    scale = d ** -0.5
    n_blocks = s // Q_BLOCK
    qb = q.astype(jnp.float32).reshape(b, h, n_blocks, Q_BLOCK, d).transpose(2, 0, 1, 3, 4)
    kf = k.astype(jnp.float32)
    vf = v.astype(jnp.float32)
    kpos = jnp.arange(s)

    def block(args):
        qi, start = args
        z = jnp.einsum('bhqd,bhkd->bhqk', qi, kf) * scale
        qpos = start + jnp.arange(Q_BLOCK)
        mask = kpos[None, :] < qpos[:, None]
        log_beta = jax.nn.log_sigmoid(z)
        log_one_minus = jnp.where(mask, jax.nn.log_sigmoid(-z), 0.0)
        tail = lax.cumsum(log_one_minus, axis=3, reverse=True) - log_one_minus
        w = jnp.where(mask, jnp.exp(log_beta + tail), 0.0)
        return jnp.einsum('bhqk,bhkd->bhqd', w, vf)

    starts = jnp.arange(n_blocks, dtype=jnp.int32) * Q_BLOCK
    out = lax.map(block, (qb, starts))
    return out.transpose(1, 2, 0, 3, 4).reshape(b, h, s, d)


def setup_inputs(seed: int = 0) -> dict:
    key = jax.random.key(seed)
    ks = jax.random.split(key, 20)

    def w(k, shape, fan_in):
        return jax.random.normal(k, shape, jnp.float32) * (fan_in ** -0.5)

    def gain(k, shape):
        return 1.0 + 0.02 * jax.random.normal(k, shape, jnp.float32)

    x = jax.random.normal(ks[0], (BATCH, SEQ, D_MODEL), jnp.float32)
    mem = jax.random.normal(ks[1], (BATCH, MEM_LEN, D_MODEL), jnp.float32)
    offset = jax.random.randint(ks[2], (BATCH,), 0, 1024, dtype=jnp.int32)
    positions = (offset[:, None] + jnp.arange(SEQ, dtype=jnp.int32)[None, :]).astype(jnp.int32)
    return {
        "x": x,
        "mem": mem,
        "positions": positions,
        "g_mix": gain(ks[3], (DEPTH, D_MODEL)),
        "w_in": w(ks[4], (DEPTH, D_MODEL, IN_COLS), D_MODEL),
        "ret_gn_g": gain(ks[5], (DEPTH, RET_HEADS, RET_DV)),
        "sb_q_g": gain(ks[6], (DEPTH, SB_HEADS, SB_DH)),
        "sb_k_g": gain(ks[7], (DEPTH, SB_HEADS, SB_DH)),
        "w_out": w(ks[8], (DEPTH, MIX_WIDTH, D_MODEL), MIX_WIDTH),
        "g_xattn": gain(ks[9], (DEPTH, D_MODEL)),
        "g_mem": gain(ks[10], (DEPTH, D_MODEL)),
        "w_xq": w(ks[11], (DEPTH, D_MODEL, D_MODEL), D_MODEL),
        "w_xkv": w(ks[12], (DEPTH, D_MODEL, 2 * D_MODEL), D_MODEL),
        "xq_g": gain(ks[13], (DEPTH, X_HEADS, X_DH)),
        "xk_g": gain(ks[14], (DEPTH, X_HEADS, X_DH)),
        "w_xo": w(ks[15], (DEPTH, D_MODEL, D_MODEL), D_MODEL),
        "g_mlp": gain(ks[16], (DEPTH, D_MODEL)),
        "w_up": w(ks[17], (DEPTH, D_MODEL, D_FF), D_MODEL),
        "w_down": w(ks[18], (DEPTH, D_FF, D_MODEL), D_FF),
    }


def reference(x, mem, positions, g_mix, w_in, ret_gn_g, sb_q_g, sb_k_g, w_out,
              g_xattn, g_mem, w_xq, w_xkv, xq_g, xk_g, w_xo, g_mlp, w_up, w_down):
    R, S = RET_WIDTH, SB_WIDTH
    for layer in range(DEPTH):
        h = rms_norm(x, g_mix[layer])
        proj = h @ w_in[layer].astype(jnp.float32)
        rq, rk, rv, rg, sq, sk, sv = jnp.split(
            proj, [R, 2 * R, 3 * R, 4 * R, 4 * R + S, 4 * R + 2 * S], axis=-1)

        rq = rotary(split_heads(rq, RET_HEADS), positions)
        rk = rotary(split_heads(rk, RET_HEADS), positions) * (RET_DK ** -0.5)
        ro = chunkwise_retention(rq, rk, split_heads(rv, RET_HEADS))
        ro = rms_norm(ro, ret_gn_g[layer][None, :, None, :])
        ro = merge_heads(ro) * jax.nn.silu(rg)

        sq = rms_norm(split_heads(sq, SB_HEADS), sb_q_g[layer][None, :, None, :])
        sk = rms_norm(split_heads(sk, SB_HEADS), sb_k_g[layer][None, :, None, :])
        so = merge_heads(stick_breaking_attention(sq, sk, split_heads(sv, SB_HEADS)))

        mix = jnp.concatenate([ro, so], axis=-1)
        x = x + (mix @ w_out[layer].astype(jnp.float32)).astype(x.dtype)

        hx = rms_norm(x, g_xattn[layer])
        m = rms_norm(mem, g_mem[layer])
        xq = rms_norm(split_heads(hx @ w_xq[layer].astype(jnp.float32), X_HEADS),
                      xq_g[layer][None, :, None, :])
        xk, xv = jnp.split(m @ w_xkv[layer].astype(jnp.float32), 2, axis=-1)
        xk = rms_norm(split_heads(xk, X_HEADS), xk_g[layer][None, :, None, :])
        xv = split_heads(xv, X_HEADS)
        scores = jnp.einsum('bhqd,bhkd->bhqk', xq, xk) * (X_DH ** -0.5)
        probs = jax.nn.softmax(scores, axis=-1)
        xo = merge_heads(jnp.einsum('bhqk,bhkd->bhqd', probs, xv))
        x = x + (xo @ w_xo[layer].astype(jnp.float32)).astype(x.dtype)

        hm = rms_norm(x, g_mlp[layer])
        up = jnp.square(jax.nn.relu(hm @ w_up[layer].astype(jnp.float32)))
        x = x + (up @ w_down[layer].astype(jnp.float32)).astype(x.dtype)
    return x
```

```python
import numpy as np
import ml_dtypes
from contextlib import ExitStack
import concourse.bass as bass
import concourse.mybir as mybir
from concourse.bass_utils import run_bass_kernel_spmd

F32 = mybir.dt.float32
BF16 = mybir.dt.bfloat16
I32 = mybir.dt.int32
AF = mybir.ActivationFunctionType
ALU = mybir.AluOpType
AX = mybir.AxisListType

D = 1024
SEQ = 4096
NB = 32
EPS = 1e-6
NEG = -30000.0
ENGS = ['pe', 'act', 'dve', 'pool', 'sp']


class Op:
    __slots__ = ('eng', 'fn', 'idx', 'waits', 'signal', 'sigval', 'dma', 'dma_val')


class Sched:
    def __init__(self):
        self.ops = {e: [] for e in ENGS}
        self.lastw = {}
        self.readers = {}
        self.known = {e: {} for e in ENGS}
        self.slots = {}
        self.const_keys = []
        self.synced = set()

    def add(self, eng, fn, reads=(), writes=(), slot=None, ndma=1):
        op = Op()
        op.eng = eng
        op.fn = fn
        op.idx = len(self.ops[eng])
        op.waits = []
        op.signal = False
        op.sigval = 0
        op.dma = slot
        op.dma_val = 0
        reads = list(reads)
        writes = list(writes)
        if eng not in self.synced and self.const_keys and slot is None:
            self.synced.add(eng)
            reads = reads + self.const_keys
        if slot is not None:
            self.slots[slot] = self.slots.get(slot, 0) + 16 * ndma
            op.dma_val = self.slots[slot]
        deps = []
        for k in reads:
            w = self.lastw.get(k)
            if w is not None:
                deps.append((w, 'raw'))
        for k in writes:
            w = self.lastw.get(k)
            if w is not None:
                deps.append((w, 'waw'))
            for r in self.readers.get(k, {}).values():
                deps.append((r, 'war'))
        best = {}
        for d, kind in deps:
            if d is op:
                continue
            if d.dma is None:
                if d.eng == eng:
                    if eng == 'pe':
                        continue
                src = d.eng
                val = d.idx
            else:
                src = ('dma', d.dma)
                val = d.dma_val
            if val <= self.known[eng].get(src, -1):
                continue
            if src not in best or val > best[src][0]:
                best[src] = (val, d)
        for src, (val, d) in best.items():
            self.known[eng][src] = val
            if d.dma is None:
                d.signal = True
            op.waits.append(d)
        for k in reads:
            self.readers.setdefault(k, {})[(eng, slot)] = op
        for k in writes:
            self.lastw[k] = op
            self.readers[k] = {}
        self.ops[eng].append(op)
        return op

    def emit(self, nc, es):
        for e in ENGS:
            c = 0
            for op in self.ops[e]:
                if op.signal:
                    c += 1
                    op.sigval = c
        sems = {e: es.enter_context(nc.semaphore('s_' + e)) for e in ENGS}
        dsem = {s: es.enter_context(nc.semaphore('d_%d' % i)) for i, s in enumerate(self.slots)}

        def run(e, eng):
            for op in self.ops[e]:
                for d in op.waits:
                    if d.dma is None:
                        eng.wait_ge(sems[d.eng], d.sigval)
                    else:
                        eng.wait_ge(dsem[d.dma], d.dma_val)
                ins = op.fn(eng)
                if ins is None:
                    continue
                if op.dma is not None:
                    if not isinstance(ins, (list, tuple)):
                        ins = [ins]
                    for i in ins:
                        i.then_inc(dsem[op.dma], 16)
                elif op.signal:
                    ins.then_inc(sems[e], 1)

        with nc.Block() as block:
            @block.tensor
            def _(eng):
                run('pe', eng)

            @block.scalar
            def _(eng):
                run('act', eng)

            @block.vector
            def _(eng):
                run('dve', eng)

            @block.gpsimd
            def _(eng):
                run('pool', eng)

            @block.sync
            def _(eng):
                run('sp', eng)


def build_program(stop=None):
    nc = bass.Bass("TRN2", target_bir_lowering=False)
    es = ExitStack()
    S = Sched()
    dbgd = nc.dram_tensor("dbg", [128, 16384], F32, kind="ExternalOutput").ap() if stop else None

    def finish(dumps):
        for i, (ap, key, c0, n) in enumerate(dumps):
            if ap.dtype != F32:
                S.add('dve', lambda e, ap=ap, c0=c0, n=n: e.tensor_copy(out=dbgs[:, c0:c0 + n], in_=ap), [key], ['dbgs%d' % i])
                S.add('sp', lambda e, c0=c0, n=n: [e.dma_start(out=dbgd[:, c0:c0 + n], in_=dbgs[:, c0:c0 + n])], ['dbgs%d' % i], ['dbgout'], slot='dbg%d' % i)
            else:
                S.add('sp', lambda e, ap=ap, c0=c0, n=n: [e.dma_start(out=dbgd[:, c0:c0 + n], in_=ap)], [key], ['dbgout'], slot='dbg%d' % i)
        S.add('sp', lambda e: None, ['dbgout'], [])
        return nc, es, S

    def din(name, shape, dt):
        return nc.dram_tensor(name, list(shape), dt, kind="ExternalInput").ap()

    xv = din("xv", [SEQ, D], F32)
    posd = din("pos", [128, NB], I32)
    dbiasd = din("dbias", [128, 1], F32)
    memd = din("mem", [256, D], F32)
    w_in = din("w_in", [D, 3584], F32)
    w_out = din("w_out", [D, D], F32)
    w_xq = din("w_xq", [D, D], F32)
    w_xkv = din("w_xkv", [D, 2 * D], F32)
    w_xo = din("w_xo", [D, D], F32)
    w_up = din("w_up", [D, 4 * D], F32)
    w_down = din("w_down", [4 * D, D], F32)
    gcols_d = din("gcols", [128, 60], F32)
    identd = din("ident", [128, 128], BF16)
    negUd = din("negU", [128, 128], BF16)
    negOd = din("negO", [128, 128], BF16)
    cmaskd = din("cmask", [128, 128], BF16)
    decd = din("decayT", [128, 512], F32)
    qdecd = din("qdecT", [128, 512], F32)
    kdecd = din("kdec", [128, 512], F32)
    invfd = din("invf", [128, 64], F32)
    outd = nc.dram_tensor("out", [2048, D], F32, kind="ExternalOutput").ap()

    def sb(name, shape, dt, stack=None):
        return (stack or es).enter_context(nc.sbuf_tensor('s_' + name, list(shape), dt))

    ps = es.enter_context(nc.psum_tensor("ps", [128, 8, 512], F32))
    dbgs = sb("dbgs", [128, 4096], F32) if stop else None

    def bank(b):
        return ps[:, b, :]

    def bankbf(b):
        return ps[:, b, :].bitcast(BF16)

    def bk(b):
        return 'B%d' % b

    WB = sb("WB", [128, 28672], BF16)
    KV = sb("KV", [128, 32768], BF16)
    mixT = sb("mixT", [128, 8, 2048], BF16)
    XT = [sb("XT%d" % i, [128, 1024], F32) for i in range(2)]
    HB = [sb("HB%d" % i, [128, 1024], BF16) for i in range(2)]
    FT = [sb("FT%d" % i, [128, 8, 128], BF16) for i in range(2)]
    SCR = sb("SCR", [128, 1024], F32)
    SCR2 = sb("SCR2", [128, 1024], F32)
    st = sb("st", [128, 256], F32)
    SL = {"s": 0}
    ident = sb("ident", [128, 128], BF16)
    negU = sb("negU", [128, 128], BF16)
    negO = sb("negO", [128, 128], BF16)
    cmask = sb("cmask", [128, 128], BF16)
    zeros = sb("zeros", [128, 512], BF16)
    gcols = sb("gcols", [128, 60], F32)
    dbias = sb("dbias", [128, 1], F32)
    invf = sb("invf", [128, 64], F32)
    posi = sb("posi", [128, NB], I32)
    posf = sb("posf", [128, NB], F32)
    epsb = sb("epsb", [128, 1], F32)
    oneb = sb("oneb", [128, 1], F32)
    xkT = sb("xkT", [128, 8, 256], BF16)
    xvb = sb("xvb", [128, 2, 1024], BF16)
    G_MIX, G_XA, G_MLP, G_MEM, G_XQ, G_XK, G_GN, G_SQ, G_SK = 0, 8, 16, 24, 32, 40, 48, 52, 56

    def dma(eng, pairs, slot, reads=(), writes=()):
        if not isinstance(pairs, list):
            pairs = [pairs]
        S.add(eng, lambda e: [e.dma_start(out=o, in_=i) for (o, i) in pairs], reads, writes, slot=slot, ndma=len(pairs))

    def mm(out, lhsT, rhs, start, stop, reads, writes, skip=False):
        S.add('pe', lambda e: e.matmul(out, lhsT, rhs, start=start, stop=stop, skip_group_check=skip), reads, writes)

    def tr(out, in_, reads, writes):
        S.add('pe', lambda e: e.transpose(out, in_, ident[:]), list(reads) + ['ident'], writes)

    def act(out, in_, func, reads, writes, bias=None, scale=None, accum_out=None):
        kw = {}
        if bias is not None:
            kw['bias'] = bias
        if scale is not None:
            kw['scale'] = scale
        if accum_out is not None:
            kw['accum_out'] = accum_out
        S.add('act', lambda e: e.activation(out=out, in_=in_, func=func, **kw), reads, writes)

    def tt(eng, out, in0, in1, op, reads, writes):
        S.add(eng, lambda e: e.tensor_tensor(out=out, in0=in0, in1=in1, op=op), reads, writes)

    def ts(eng, out, in0, s1, op0, reads, writes, s2=None, op1=None):
        if op1 is None:
            S.add(eng, lambda e: e.tensor_scalar(out=out, in0=in0, scalar1=s1, scalar2=None, op0=op0), reads, writes)
        else:
            S.add(eng, lambda e: e.tensor_scalar(out=out, in0=in0, scalar1=s1, scalar2=s2, op0=op0, op1=op1), reads, writes)

    def stt(out, in0, scalar, in1, op0, op1, reads, writes):
        S.add('dve', lambda e: e.scalar_tensor_tensor(out=out, in0=in0, scalar=scalar, in1=in1, op0=op0, op1=op1), reads, writes)

    import os
    VAR = os.environ.get('KVAR', '')

    def cp(eng, out, in_, reads, writes):
        if eng == 'act' and in_.dtype == BF16:
            eng = 'dve'
        if eng == 'pool' and 'P' in VAR:
            eng = 'dve'
        if eng == 'act':
            S.add('act', lambda e: e.copy(out=out, in_=in_), reads, writes)
        else:
            S.add(eng, lambda e: e.tensor_copy(out=out, in_=in_), reads, writes)

    def reduce_(out, in_, op, reads, writes):
        S.add('dve', lambda e: e.tensor_reduce(out=out, in_=in_, axis=AX.X, op=op), reads, writes)

    def rstd_from_ss(ss, tmp, out, inv_n, key_ss, key_tmp, key_out):
        act(tmp, ss, AF.Ln, [key_ss], [key_tmp], bias=EPS, scale=inv_n)
        act(out, tmp, AF.Exp, [key_tmp], [key_out], scale=-0.5)

    nbar = [0]

    def barrier():
        n = nbar[0]
        nbar[0] += 1
        keys = ['bar%d_%s' % (n, e) for e in ('pe', 'act', 'dve', 'pool')]
        S.add('pe', lambda e: e.matmul(bank(7)[:, 0:1], ident[:], zeros[:, 0:1], start=True, stop=True),
              ['ident', 'zeros'], [keys[0], bk(7)])
        S.add('act', lambda e: e.copy(out=st[:, 121:122], in_=st[:, 120:121]), [], [keys[1]])
        S.add('dve', lambda e: e.tensor_copy(out=st[:, 123:124], in_=st[:, 122:123]), [], [keys[2]])
        S.add('pool', lambda e: e.memset(st[:, 124:125], 0.0), [], [keys[3]])
        for e_ in ('pe', 'act', 'dve', 'pool', 'sp'):
            S.add(e_, lambda e: None, keys, [])

    ck = []

    def cload(dst, src, key):
        dma('sp', (dst, src), 'const', [], [key])
        ck.append(key)

    cload(ident[:], identd, 'ident')
    cload(negU[:], negUd, 'negU')
    cload(negO[:], negOd, 'negO')
    cload(cmask[:], cmaskd, 'cmask')
    cload(gcols[:], gcols_d, 'gcols')
    cload(dbias[:], dbiasd, 'dbias')
    cload(invf[:], invfd, 'invf')
    cload(posi[:], posd, 'posi')
    S.add('pool', lambda e: e.memset(zeros[:], 0.0), [], ['zeros'])
    S.add('pool', lambda e: e.memset(epsb[:], EPS), [], ['epsb'])
    S.add('pool', lambda e: e.memset(oneb[:], 1.0), [], ['oneb'])
    S.add('pool', lambda e: e.memset(st[:], 0.0), [], ['st0'])
    S.const_keys = list(dict.fromkeys(ck)) + ['zeros', 'epsb', 'oneb', 'st0']

    WR = WB[:, 0:16384].rearrange("p (k n) -> p k n", k=8)
    WS = WB[:, 16384:28672].rearrange("p (k n) -> p k n", k=8)
    WO = WB[:, 0:8192].rearrange("p (k n) -> p k n", k=8)
    WQ = WB[:, 8192:16384].rearrange("p (k n) -> p k n", k=8)
    WX = WB[:, 16384:24576].rearrange("p (k n) -> p k n", k=8)
    RrF = WB[:, 24576:28672].bitcast(F32)
    w_in_v = w_in.rearrange("(k p) n -> p k n", p=128)
    w_out_v = w_out.rearrange("(k p) n -> p k n", p=128)
    w_xq_v = w_xq.rearrange("(k p) n -> p k n", p=128)
    w_xo_v = w_xo.rearrange("(k p) n -> p k n", p=128)
    w_xkv_v = w_xkv.rearrange("(k p) n -> p k n", p=128)
    w_up_v = w_up.rearrange("(k p) n -> p k n", p=128)
    w_down_v = w_down.rearrange("(f p) n -> p f n", p=128)
    WKV = KV[:, 0:16384].rearrange("p (k n) -> p k n", k=8)
    dma('pool', [(WKV[:, kc, :], w_xkv_v[:, kc, :]) for kc in range(8)], 'wkv', [], ['KVa'])
    dma('pool', [(WR[:, kc, :], w_in_v[:, kc, 0:2048]) for kc in range(8)], 'wr', [], ['WBa'])
    dma('pool', [(WS[:, kc, :], w_in_v[:, kc, 2048:3584]) for kc in range(8)], 'ws', [], ['WBc'])

    if stop == 'const':
        return finish([(gcols[:], 'gcols', 0, 60), (WR[:, 0, 0:512], 'WBa', 64, 512)])
    cnt = {'xt': 0, 'hb': 0, 'ft': 0}

    def next_hb():
        i = cnt['hb'] % 2
        cnt['hb'] += 1
        return HB[i], 'HB%d' % i

    def next_ft():
        i = cnt['ft'] % 2
        cnt['ft'] += 1
        return FT[i], 'FT%d' % i

    def transpose_gain(srcb, src_key, nk, gc, dstT, dst_key):
        pb_ = 7 - SL['s']
        ptb = bankbf(pb_)
        for k in range(nk):
            tr(ptb[:, k * 128:(k + 1) * 128], srcb[:, k * 128:(k + 1) * 128], [src_key], [bk(pb_)])
        pv = ptb[:, 0:nk * 128].rearrange("p (k t) -> p k t", k=nk)
        if gc is None:
            cp('act', dstT, pv, [bk(pb_)], [dst_key])
        else:
            tt('dve', dstT, pv, gc.unsqueeze(2).to_broadcast([128, nk, 128]), ALU.mult, [bk(pb_), 'gcols'], [dst_key])

    def norm_T(src_ap, src_key, gcol0, dstT, dst_key):
        hb, hk = next_hb()
        so = 128 * SL['s']
        sx = '_%d' % SL['s']
        act(hb[:], src_ap, AF.Square, [src_key], [hk, 'st_ss' + sx], accum_out=st[:, so:so + 1])
        rstd_from_ss(st[:, so:so + 1], st[:, so + 1:so + 2], st[:, so + 2:so + 3], 1.0 / 1024, 'st_ss' + sx, 'st_ln' + sx, 'st_rs' + sx)
        ts('dve', hb[:], src_ap, st[:, so + 2:so + 3], ALU.mult, [src_key, 'st_rs' + sx], [hk])
        transpose_gain(hb, hk, 8, gcols[:, gcol0:gcol0 + 8], dstT, dst_key)

    def front(blk):
        i = cnt['xt'] % 2
        cnt['xt'] += 1
        dma('sp', (XT[i][:], xv[blk * 128:(blk + 1) * 128, :]), 'xt%d' % i, [], ['XT%d' % i])
        ft, fk = next_ft()
        norm_T(XT[i][:], 'XT%d' % i, G_MIX, ft[:], fk)
        return ft, fk

    def proj(hT, hkey, W, wkeys, c0, b):
        for kc in range(8):
            mm(bank(b), hT[:, kc, :], W[:, kc, c0:c0 + 512], kc == 0, kc == 7, [hkey] + wkeys, [bk(b)])

    def is_own(blk):
        return (blk // 4) % 2 == 1

    def own_col(blk):
        return (blk // 8) * 512 + (blk % 4) * 128

    KVf = KV[:, :].bitcast(F32)
    COS = KVf[:, 8192:10240].rearrange("p (b d) -> p b d", b=NB)
    SIN = KVf[:, 10240:12288].rearrange("p (b d) -> p b d", b=NB)
    decT = KVf[:, 12288:12800]
    qdecT = KVf[:, 12800:13312]
    kdec = KVf[:, 13312:13824]
    Sf = KVf[:, 13824:14336]
    KVb16 = KV[:, 28672:32768]
    Sb = KVb16[:, 0:512]
    rqb = KVb16[:, 512:1024]
    rkb = KVb16[:, 1024:1536]
    rkd = KVb16[:, 1536:2048]
    rvb = KVb16[:, 2048:2560]
    rqT = KVb16[:, 2560:3072]
    rqTd = KVb16[:, 3072:3584]
    rkT = KVb16[:, 3584:4096]
    sTm = HB[0]

    SCRS = [(SCR[:], SCR2[:], ['SCRa', 'SCRb'], ['SCR2a', 'SCR2b']),
            (KVf[:, 10240:11264], KVf[:, 11264:12288], ['SCRx'], ['SCR2x'])]

    def head4_norm(b0, dstb, dkey):
        s_ = SL['s']
        A_, B_, ka, kb_ = SCRS[s_]
        so = 128 * s_
        sx = '_%d' % s_
        cp('act', A_[:, 0:512], bank(b0), [bk(b0)], ka)
        cp('act', A_[:, 512:1024], bank(b0 + 1), [bk(b0 + 1)], ka)
        tt('dve', B_, A_, A_, ALU.mult, ka, kb_)
        reduce_(st[:, so + 48:so + 52], B_.rearrange("p (h d) -> p h d", h=4), ALU.add, kb_, ['st_h' + sx])
        rstd_from_ss(st[:, so + 48:so + 52], st[:, so + 52:so + 56], st[:, so + 56:so + 60], 1.0 / 256, 'st_h' + sx, 'st_hl' + sx, 'st_hr' + sx)
        tt('dve', dstb.rearrange("p (h d) -> p h d", h=4), A_.rearrange("p (h d) -> p h d", h=4),
           st[:, so + 56:so + 60].unsqueeze(2).to_broadcast([128, 4, 256]), ALU.mult, ka + ['st_hr' + sx], [dkey])

    def proj2(hT, hkey, W, wkeys, c0, b0):
        for nh in range(2):
            for kc in range(8):
                mm(bank(b0 + nh), hT[:, kc, :], W[:, kc, c0 + nh * 512:c0 + (nh + 1) * 512], kc == 0, kc == 7,
                   [hkey] + wkeys, [bk(b0 + nh)])

    TWO_PI = 2.0 * np.pi
    cp('dve', posf[:], posi[:], ['posi'], ['posf'])
    KI = XT[0][:].bitcast(I32)
    for half in range(2):
        bs = slice(half * 16, half * 16 + 16)
        ANG = SCR[:].rearrange("p (b d) -> p b d", b=16)
        A2 = SCR2[:].rearrange("p (b d) -> p b d", b=16)
        KIv = KI.rearrange("p (b d) -> p b d", b=16)
        tt('dve', ANG, posf[:, bs].unsqueeze(2).to_broadcast([128, 16, 64]),
           invf[:, :].unsqueeze(1).to_broadcast([128, 16, 64]), ALU.mult, ['posf', 'invf'], ['ANG'])
        for (dst, shift, key) in ((SIN[:, bs, :], 0.0, 'SIN'), (COS[:, bs, :], float(np.pi / 2), 'COS')):
            ts('dve', A2, ANG, shift, ALU.add, ['ANG'], ['A2'], s2=1.0 / TWO_PI, op1=ALU.mult)
            cp('dve', KIv, A2, ['A2'], ['KI'])
            cp('dve', A2, KIv, ['KI'], ['A2'])
            c1 = 6.28125
            c2 = float(TWO_PI - 6.28125)
            stt(dst, A2, -c1, ANG, ALU.mult, ALU.add, ['A2', 'ANG'], [key])
            stt(dst, A2, -c2, dst, ALU.mult, ALU.add, ['A2', key], [key])
            if shift != 0.0:
                ts('dve', dst, dst, shift, ALU.add, [key], [key])
            ts('dve', A2, dst, float(np.pi), ALU.is_gt, [key], ['A2'], s2=-TWO_PI, op1=ALU.mult)
            tt('dve', dst, dst, A2, ALU.add, [key, 'A2'], [key])
            ts('dve', A2, dst, float(-np.pi), ALU.is_lt, [key], ['A2'], s2=TWO_PI, op1=ALU.mult)
            tt('dve', dst, dst, A2, ALU.add, [key, 'A2'], [key])
            act(dst, dst, AF.Sin, [key], [key])
    dma('sp', [(decT, decd), (qdecT, qdecd), (kdec, kdecd)], 'rconst', [], ['decT', 'qdecT', 'kdec'])
    S.add('pool', lambda e: e.memset(Sf, 0.0), [], ['Sf'])
    S.add('pool', lambda e: e.memset(Sb, 0.0), [], ['Sb'])
    barrier()
    if stop == 'rot':
        return finish([(COS.rearrange('p b d -> p (b d)'), 'COS', 0, 2048), (SIN.rearrange('p b d -> p (b d)'), 'SIN', 2048, 2048)])

    for mb in range(2):
        i = cnt['xt'] % 2
        cnt['xt'] += 1
        dma('sp', (XT[i][:], memd[mb * 128:(mb + 1) * 128, :]), 'xt%d' % i, [], ['XT%d' % i])
        ft, fk = next_ft()
        norm_T(XT[i][:], 'XT%d' % i, G_MEM, ft[:], fk)
        proj2(ft, fk, WKV, ['KVa'], 0, 0)
        hb, hk = next_hb()
        head4_norm(0, hb[:], hk)
        transpose_gain(hb, hk, 8, gcols[:, G_XK:G_XK + 8], xkT[:, :, mb * 128:(mb + 1) * 128], 'xkT')
        proj2(ft, fk, WKV, ['KVa'], 1024, 2)
        cp('act', xvb[:, mb, 0:512], bank(2), [bk(2)], ['xvb'])
        cp('act', xvb[:, mb, 512:1024], bank(3), [bk(3)], ['xvb'])
    barrier()
    if stop == 'memkv':
        return finish([(xkT[:].rearrange('p k m -> p (k m)'), 'xkT', 0, 2048), (xvb[:].rearrange('p k m -> p (k m)'), 'xvb', 2048, 2048)])

    CDEC = [float(np.exp(np.float32(np.log1p(np.float32(-(2.0 ** (-5.0 - h))))) * np.float32(128.0))) for h in range(4)]
    sTm = KV[:, 0:512]
    rob = KV[:, 512:1024]

    def rotary(b, blk, dst, dkey):
        src = bank(b).rearrange("p (h two d) -> p h two d", h=4, two=2)
        dv = dst.rearrange("p (h two d) -> p h two d", h=4, two=2)
        cb = COS[:, blk, :].unsqueeze(1).to_broadcast([128, 4, 64])
        sbb = SIN[:, blk, :].unsqueeze(1).to_broadcast([128, 4, 64])
        t = SCR[:].rearrange("p (a h d) -> p a h d", a=4, h=4)
        tt('dve', t[:, 0], src[:, :, 0, :], cb, ALU.mult, [bk(b), 'COS'], ['SCRa'])
        tt('dve', t[:, 1], src[:, :, 1, :], sbb, ALU.mult, [bk(b), 'SIN'], ['SCRa'])
        tt('dve', t[:, 2], src[:, :, 1, :], cb, ALU.mult, [bk(b), 'COS'], ['SCRb'])
        tt('dve', t[:, 3], src[:, :, 0, :], sbb, ALU.mult, [bk(b), 'SIN'], ['SCRb'])
        tt('dve', dv[:, :, 0, :], t[:, 0], t[:, 1], ALU.subtract, ['SCRa'], [dkey])
        tt('dve', dv[:, :, 1, :], t[:, 2], t[:, 3], ALU.add, ['SCRb'], [dkey])

    nxt_front = front(0)
    for blk in range(NB):
        own = is_own(blk)
        hT, hk = nxt_front
        if blk + 1 < NB:
            nxt_front = front(blk + 1)
        wk = ['WBa']
        if own:
            proj(hT, hk, WR, wk, 0, 0)
            rotary(0, blk, rqb, 'rqb')
            if stop == 'R%d%s' % (blk, 'a'):
                return finish([(Sf, 'Sf', 0, 512)])
        proj(hT, hk, WR, wk, 512, 1)
        rotary(1, blk, SCR2[:, 0:512], 'SCR2a')
        tt('pool', rkd, SCR2[:, 0:512], kdec, ALU.mult, ['SCR2a', 'kdec'], ['rkd'])
        if own:
            cp('pool', rkb, SCR2[:, 0:512], ['SCR2a'], ['rkb'])
        proj(hT, hk, WR, wk, 1024, 2)
        cp('act', rvb, bank(2), [bk(2)], ['rvb'])
        if own:
            proj(hT, hk, WR, wk, 1536, 3)
            act(SCR2[:, 512:1024], bank(3), AF.Silu, [bk(3)], ['SCR2b'])
            ptb = bankbf(7)
            for h in range(4):
                tr(ptb[:, h * 128:(h + 1) * 128], rqb[:, h * 128:(h + 1) * 128], ['rqb'], [bk(7)])
            for h in range(4):
                tr(ptb[:, 512 + h * 128:512 + (h + 1) * 128], rkb[:, h * 128:(h + 1) * 128], ['rkb'], [bk(7)])
            cp('act', rqT, ptb[:, 0:512], [bk(7)], ['rqT'])
            tt('dve', rqTd, ptb[:, 0:512], qdecT, ALU.mult, [bk(7), 'qdecT'], ['rqTd'])
            cp('act', rkT, ptb[:, 512:1024], [bk(7)], ['rkT'])
            if stop == 'R%d%s' % (blk, 'b'):
                return finish([(Sf, 'Sf', 0, 512)])
            for h in range(4):
                hs = slice(h * 128, (h + 1) * 128)
                mm(bank(4)[:, hs], rkT[:, hs], rqT[:, hs], True, True, ['rkT', 'rqT'], [bk(4)])
            tt('dve', sTm, bank(4), decT, ALU.mult, [bk(4), 'decT'], ['sTm'])
            if stop == 'R%d%s' % (blk, 'c'):
                return finish([(Sf, 'Sf', 0, 512)])
            for h in range(4):
                hs = slice(h * 128, (h + 1) * 128)
                mm(bank(5)[:, hs], sTm[:, hs], rvb[:, hs], True, False, ['sTm', 'rvb'], [bk(5)])
                mm(bank(5)[:, hs], rqTd[:, hs], Sb[:, hs], False, True, ['rqTd', 'Sb'], [bk(5)])
            cp('act', SCR[:, 0:512], bank(5), [bk(5)], ['SCRa'])
            tt('dve', SCR[:, 512:1024], SCR[:, 0:512], SCR[:, 0:512], ALU.mult, ['SCRa'], ['SCRb'])
            reduce_(st[:, 8:12], SCR[:, 512:1024].rearrange("p (h d) -> p h d", h=4), ALU.add, ['SCRb'], ['st_g'])
            rstd_from_ss(st[:, 8:12], st[:, 12:16], st[:, 16:20], 1.0 / 128, 'st_g', 'st_gl', 'st_gr')
            if stop == 'R%d%s' % (blk, 'd'):
                return finish([(Sf, 'Sf', 0, 512)])
            for h in range(4):
                hs = slice(h * 128, (h + 1) * 128)
                stt(rob[:, hs], SCR[:, hs], st[:, 16 + h:17 + h], SCR2[:, 512 + h * 128:512 + (h + 1) * 128],
                    ALU.mult, ALU.mult, ['SCRa', 'st_gr', 'SCR2b'], ['rob'])
            c0 = own_col(blk)
            transpose_gain(rob, 'rob', 4, gcols[:, G_GN:G_GN + 4], mixT[:, 0:4, c0:c0 + 128], 'mixT')
            if stop == 'R%d%s' % (blk, 'e'):
                return finish([(Sf, 'Sf', 0, 512)])
        for h in range(4):
            hs = slice(h * 128, (h + 1) * 128)
            mm(bank(6)[:, hs], rkd[:, hs], rvb[:, hs], True, True, ['rkd', 'rvb'], [bk(6)])
        for h in range(4):
            hs = slice(h * 128, (h + 1) * 128)
            stt(Sf[:, hs], Sf[:, hs], CDEC[h], bank(6)[:, hs], ALU.mult, ALU.add, ['Sf', bk(6)], ['Sf'])
        cp('pool', Sb, Sf, ['Sf'], ['Sb'])
        if stop == 'R%d' % blk:
            return finish([(Sf, 'Sf', 2048, 512), (rkd, 'rkd', 2560, 512), (rvb, 'rvb', 3072, 512)])
    barrier()
    if stop == 'R':
        return finish([(mixT[:, k, 0:1024], 'mixT', k * 1024, 1024) for k in range(4)])
    dma('pool', [(WO[:, kc, :], w_out_v[:, kc, :]) for kc in range(8)], 'wo', [], ['WBa'])
    dma('pool', [(WQ[:, kc, :], w_xq_v[:, kc, :]) for kc in range(8)], 'wq', [], ['WBa'])

    esS = ExitStack()
    skT = KV[:, 0:16384].rearrange("p (g t) -> p g t", g=4)
    svv = KV[:, 16384:32768].rearrange("p (b c) -> p b c", b=NB)
    sqT = sb("sqT", [128, 4, 512], BF16, esS)
    Ebuf = [SCR[:, 0:512], SCR[:, 512:1024]]
    Ekey = ['SCRa', 'SCRb']
    Lb = [[sb("Lb%d%d" % (i, j), [128, 512], BF16, esS) for j in range(2)] for i in range(2)]
    wT = [[sb("wT%d%d" % (i, j), [128, 512], BF16, esS) for j in range(2)] for i in range(2)]
    Laccb = [[sb("Laccb%d%d" % (i, j), [128, 512], BF16, esS) for j in range(2)] for i in range(2)]
    Lacc = [SCR2[:, 0:512], SCR2[:, 512:1024]]
    Lkey = ['SCR2a', 'SCR2b']

    def qk_norm(b, scale, dstb, dkey):
        cp('act', SCR2[:, 0:512], bank(b), [bk(b)], ['SCR2a'])
        tt('dve', SCR2[:, 512:1024], SCR2[:, 0:512], SCR2[:, 0:512], ALU.mult, ['SCR2a'], ['SCR2b'])
        reduce_(st[:, 24:32], SCR2[:, 512:1024].rearrange("p (h d) -> p h d", h=8), ALU.add, ['SCR2b'], ['st_q'])
        rstd_from_ss(st[:, 24:32], st[:, 32:40], st[:, 40:48], 1.0 / 64, 'st_q', 'st_ql', 'st_qr')
        if scale != 1.0:
            ts('dve', st[:, 40:48], st[:, 40:48], scale, ALU.mult, ['st_qr'], ['st_qr'])
        tt('dve', dstb.rearrange("p (h d) -> p h d", h=8), SCR2[:, 0:512].rearrange("p (h d) -> p h d", h=8),
           st[:, 40:48].unsqueeze(2).to_broadcast([128, 8, 64]), ALU.mult, ['SCR2a', 'st_qr'], [dkey])

    nxt_front = front(0)
    for blk in range(NB):
        own = is_own(blk)
        hT, hk = nxt_front
        if blk + 1 < NB:
            nxt_front = front(blk + 1)
        wk = ['WBc']
        proj(hT, hk, WS, wk, 512, 1)
        hb, hbk = next_hb()
        qk_norm(1, 1.0, hb[:, 0:512], hbk)
        transpose_gain(hb, hbk, 4, gcols[:, G_SK:G_SK + 4], skT[:, :, blk * 128:(blk + 1) * 128], 'skT')
        proj(hT, hk, WS, wk, 1024, 2)
        cp('act', svv[:, blk, :], bank(2), [bk(2)], ['sv'])
        if own:
            proj(hT, hk, WS, wk, 0, 0)
            hb, hbk = next_hb()
            qk_norm(0, 0.125, hb[:, 0:512], hbk)
            i4 = blk % 4
            transpose_gain(hb, hbk, 4, gcols[:, G_SQ:G_SQ + 4], sqT[:, :, i4 * 128:(i4 + 1) * 128], 'sqT')
        if own and blk % 4 == 3:
            vt = blk // 4
            tcol = (vt // 2) * 512
            nsteps = 4 * vt + 4
            def geo(pr, step, g):
                kb = 4 * vt + 3 - step
                di = kb - 4 * vt
                n0 = 128 * di if di >= 0 else 0
                n0n = 128 * (di - 1) if di - 1 >= 0 else 0
                return kb, di, n0, n0n, slice(n0, 512), g % 2, step == nsteps - 1, kb < 4, g % 3

            def stA(pr, step, g):
                kb, di, n0, n0n, cs, par, last, useb, z3 = geo(pr, step, g)
                for hh in range(2):
                    pl = slice(64 * hh, 64 * hh + 64)
                    zb = hh * 3 + z3
                    mm(bank(zb)[:, cs], skT[pl, pr, kb * 128:(kb + 1) * 128], sqT[pl, pr, cs], True, di < 0,
                       ['skT', 'sqT'], [bk(zb)])
                    if di >= 0:
                        mm(bank(zb)[:, n0:n0 + 128], ident[:], cmask[:], False, True, ['ident', 'cmask'], [bk(zb)])

            def stB(pr, step, g):
                kb, di, n0, n0n, cs, par, last, useb, z3 = geo(pr, step, g)
                for hh in range(2):
                    zb = hh * 3 + z3
                    if useb:
                        act(Ebuf[hh][:, cs], bank(zb)[:, cs], AF.Exp, [bk(zb), 'dbias'], [Ekey[hh]], bias=dbias[:, 0:1])
                    else:
                        act(Ebuf[hh][:, cs], bank(zb)[:, cs], AF.Exp, [bk(zb)], [Ekey[hh]])
                for hh in range(2):
                    act(Lb[hh][par][:, cs], Ebuf[hh][:, cs], AF.Ln, [Ekey[hh]], ['L%d%d' % (hh, par)],
                        bias=1.0)
                if not last:
                    csn = slice(n0n, 512)
                    for hh in range(2):
                        tt('dve', Lacc[hh][:, cs], Lacc[hh][:, cs], Lb[hh][par][:, cs], ALU.add,
                           [Lkey[hh], 'L%d%d' % (hh, par)], [Lkey[hh]])
                    for hh in range(2):
                        cp('dve', Laccb[hh][par][:, csn], Lacc[hh][:, csn], [Lkey[hh]], ['LA%d%d' % (hh, par)])

            def stC(pr, step, g):
                kb, di, n0, n0n, cs, par, last, useb, z3 = geo(pr, step, g)
                for hh in range(2):
                    zb = hh * 3 + z3
                    mm(bank(zb)[:, cs], negU[:], Lb[hh][par][:, cs], False, True,
                       ['negU', 'L%d%d' % (hh, par)], [bk(zb)], skip=True)
                    if step > 0:
                        mm(bank(zb)[:, cs], negO[:], Laccb[hh][1 - par][:, cs], False, True,
                           ['negO', 'LA%d%d' % (hh, 1 - par)], [bk(zb)], skip=True)

            def stD(pr, step, g):
                kb, di, n0, n0n, cs, par, last, useb, z3 = geo(pr, step, g)
                for hh in range(2):
                    zb = hh * 3 + z3
                    if useb:
                        act(wT[hh][par][:, cs], bank(zb)[:, cs], AF.Exp, [bk(zb), 'dbias'], ['w%d%d' % (hh, par)],
                            bias=dbias[:, 0:1])
                    else:
                        act(wT[hh][par][:, cs], bank(zb)[:, cs], AF.Exp, [bk(zb)], ['w%d%d' % (hh, par)])

            def stE(pr, step, g):
                kb, di, n0, n0n, cs, par, last, useb, z3 = geo(pr, step, g)
                for hh in range(2):
                    mm(bank(6 + hh)[:, cs], svv[:, kb, pr * 128:(pr + 1) * 128], wT[hh][par][:, cs], False, True,
                       ['sv', 'w%d%d' % (hh, par)], [bk(6 + hh)], skip=True)


            def pair_init(pr):
                for hh in range(2):
                    mm(bank(6 + hh), negO[:], zeros[:], True, True, ['negO', 'zeros'], [bk(6 + hh)])

            def lacc_init():
                for hh in range(2):
                    S.add('pool', lambda e, hh=hh: e.memset(Lacc[hh], 0.0), [], [Lkey[hh]])

            def pair_evac(pr):
                cp('act', mixT[0:64, 4 + pr, tcol:tcol + 512], bank(6)[0:64, :], [bk(6)], ['mixT'])
                cp('dve', mixT[64:128, 4 + pr, tcol:tcol + 512], bank(7)[64:128, :], [bk(7)], ['mixT'])

            FL = [(pr, step) for pr in range(4) for step in range(nsteps)]
            G = len(FL)
            pair_init(0)
            lacc_init()
            stA(FL[0][0], FL[0][1], 0)
            for g in range(G):
                if g + 1 < G:
                    stA(FL[g + 1][0], FL[g + 1][1], g + 1)
                if g > 0 and FL[g][1] == 0:
                    lacc_init()
                stB(FL[g][0], FL[g][1], g)
                if g >= 1:
                    stD(FL[g - 1][0], FL[g - 1][1], g - 1)
                stC(FL[g][0], FL[g][1], g)
                if g >= 1:
                    stE(FL[g - 1][0], FL[g - 1][1], g - 1)
                    if FL[g - 1][1] == nsteps - 1:
                        pair_evac(FL[g - 1][0])
                        pair_init(FL[g][0])
            stD(FL[G - 1][0], FL[G - 1][1], G - 1)
            stE(FL[G - 1][0], FL[G - 1][1], G - 1)
            pair_evac(FL[G - 1][0])
    barrier()
    if stop == 'S':
        return finish([(mixT[:, 4 + k, 0:1024], 'mixT', k * 1024, 1024) for k in range(4)])
    esS.close()

    dma('pool', [(WX[:, kc, :], w_xo_v[:, kc, :]) for kc in range(8)], 'wx', [], ['WBc'])
    hmT = KV[:, 0:16384].rearrange("p (k t) -> p k t", k=8)
    esO1 = ExitStack()
    FTO = [sb("FTO%d" % i, [128, 8, 128], BF16, esO1) for i in range(4)]
    XTO = [XT[0][:], XT[1][:], KVf[:, 8192:9216], KVf[:, 9216:10240]]

    def frontO(bo):
        ot, i4 = bo // 4, bo % 4
        row = (2 * ot + 1) * 4 + i4
        tc0 = bo * 128
        xi = bo % 4
        x1 = XTO[xi]
        xk1 = 'XTO%d' % xi
        dma('sp', (x1, xv[row * 128:(row + 1) * 128, :]), 'xo%d' % xi, [], [xk1])
        for nh in range(2):
            for kc in range(8):
                mm(bank(4 + nh), mixT[:, kc, tc0:tc0 + 128], WO[:, kc, nh * 512:(nh + 1) * 512], kc == 0, kc == 7,
                   ['mixT', 'WBa'], [bk(4 + nh)])
        for nh in range(2):
            tt('dve', x1[:, nh * 512:(nh + 1) * 512], bank(4 + nh), x1[:, nh * 512:(nh + 1) * 512], ALU.add,
               [bk(4 + nh), xk1], [xk1])
        ft, fk = FTO[xi], 'FTO%d' % xi
        norm_T(x1, xk1, G_XA, ft[:], fk)
        return dict(bo=bo, tc0=tc0, xi=xi, x1=x1, xk1=xk1, ft=ft, fk=fk)

    def st1(c):
        b0 = 2 * SL['s']
        proj2(c['ft'], c['fk'], WQ, ['WBa'], 0, b0)
        hb, hbk = next_hb()
        head4_norm(b0, hb[:], hbk)
        c['xqT'], c['xqk'] = next_ft()
        transpose_gain(hb, hbk, 8, gcols[:, G_XQ:G_XQ + 8], c['xqT'][:], c['xqk'])

    def st2(c):
        b0 = 2 * SL['s']
        so = 128 * SL['s']
        sx = '_%d' % SL['s']
        xqT, xqk = c['xqT'], c['xqk']
        for h in range(4):
            for dc in range(2):
                mm(bank(b0 + h // 2)[:, (h % 2) * 256:(h % 2 + 1) * 256], xqT[:, 2 * h + dc, :], xkT[:, 2 * h + dc, :],
                   dc == 0, dc == 1, [xqk, 'xkT'], [bk(b0 + h // 2)])
        for nh in range(2):
            reduce_(st[:, so + 60 + 2 * nh:so + 62 + 2 * nh], bank(b0 + nh).rearrange("p (h m) -> p h m", h=2), ALU.max,
                    [bk(b0 + nh)], ['st_mx' + sx])
        ts('dve', st[:, so + 64:so + 68], st[:, so + 60:so + 64], -1.0 / 16, ALU.mult, ['st_mx' + sx], ['st_nm' + sx])
        pb, pk = next_hb()
        for h in range(4):
            act(pb[:, h * 256:(h + 1) * 256], bank(b0 + h // 2)[:, (h % 2) * 256:(h % 2 + 1) * 256], AF.Exp,
                [bk(b0 + h // 2), 'st_nm' + sx], [pk, 'st_sm' + sx], bias=st[:, so + 64 + h:so + 65 + h], scale=1.0 / 16,
                accum_out=st[:, so + 68 + h:so + 69 + h])
        S.add('dve', lambda e: e.reciprocal(out=st[:, so + 72:so + 76], in_=st[:, so + 68:so + 72]), ['st_sm' + sx], ['st_rs4' + sx])
        c['pT'], c['pTk'] = next_ft()
        transpose_gain(pb, pk, 8, None, c['pT'][:], c['pTk'])

    def st3(c):
        b0 = 2 * SL['s']
        so = 128 * SL['s']
        sx = '_%d' % SL['s']
        pT, pTk = c['pT'], c['pTk']
        for h in range(4):
            for mc in range(2):
                mm(bank(b0 + h // 2)[:, (h % 2) * 256:(h % 2 + 1) * 256], pT[:, 2 * h + mc, :],
                   xvb[:, mc, h * 256:(h + 1) * 256], mc == 0, mc == 1, [pTk, 'xvb'], [bk(b0 + h // 2)])
        xob, xok = next_hb()
        for h in range(4):
            ts('dve', xob[:, h * 256:(h + 1) * 256], bank(b0 + h // 2)[:, (h % 2) * 256:(h % 2 + 1) * 256],
               st[:, so + 72 + h:so + 73 + h], ALU.mult, [bk(b0 + h // 2), 'st_rs4' + sx], [xok])
        c['xoT'], c['xoTk'] = next_ft()
        transpose_gain(xob, xok, 8, None, c['xoT'][:], c['xoTk'])

    def st4(c):
        b0 = 2 * SL['s']
        x1, xk1, tc0, bo = c['x1'], c['xk1'], c['tc0'], c['bo']
        proj2(c['xoT'], c['xoTk'], WX, ['WBc'], 0, b0)
        for nh in range(2):
            tt('dve', x1[:, nh * 512:(nh + 1) * 512], bank(b0 + nh), x1[:, nh * 512:(nh + 1) * 512], ALU.add,
               [bk(b0 + nh), xk1], [xk1])
        norm_T(x1, xk1, G_MLP, hmT[:, :, tc0:tc0 + 128], 'hmT')
        dma('sp', (outd[bo * 128:(bo + 1) * 128, :], x1), 'x2o%d' % c['xi'], [xk1], ['x2d%d' % bo])

    ctxs = {0: frontO(0), 1: frontO(1)}
    for pbo in range(0, 16, 2):
        if pbo + 2 < 16:
            for s_ in range(2):
                SL['s'] = s_
                ctxs[pbo + 2 + s_] = frontO(pbo + 2 + s_)
        for stage in (st1, st2, st3, st4):
            for s_ in range(2):
                SL['s'] = s_
                stage(ctxs[pbo + s_])
    SL['s'] = 0
    barrier()
    esO1.close()

    YA = mixT[:].rearrange("p k t -> p (k t)").bitcast(F32)
    YB = WB[:, 0:16384].bitcast(F32)
    Yb = [YA[:, j * 1024:(j + 1) * 1024] for j in range(8)] + [YB[:, j * 1024:(j + 1) * 1024] for j in range(8)]
    Rr = [RrF[:, 0:512], RrF[:, 512:1024]]
    esO = ExitStack()
    WDt = sb("WDt", [128, 8192], BF16, esO)
    uTc = [KV[:, 16384 + i * 8192:16384 + (i + 1) * 8192].rearrange("p (f t) -> p f t", f=4) for i in range(2)]
    WUc = [WB[:, 16384 + i * 4096:16384 + (i + 1) * 4096].rearrange("p (k n) -> p k n", k=8) for i in range(2)]
    WDc = [WDt[:, i * 4096:(i + 1) * 4096].rearrange("p (f n) -> p f n", f=4) for i in range(2)]
    dma('sp', [(Yb[j], outd[j * 128:(j + 1) * 128, :]) for j in range(16)], 'yinit',
        ['x2d%d' % j for j in range(16)], ['Y%d' % j for j in range(16)])
    ubc = [0]
    ybc = [0]
    for c in range(8):
        wb = c % 2
        dma('pool', [(WUc[wb][:, 0:4, :], w_up_v[:, 0:4, c * 512:(c + 1) * 512]),
                     (WUc[wb][:, 4:8, :], w_up_v[:, 4:8, c * 512:(c + 1) * 512])], 'wu%d' % wb, [], ['WU%d' % wb, 'WBc'])
        dma('pool', [(WDc[wb][:, 0:2, :], w_down_v[:, 4 * c:4 * c + 2, :]),
                     (WDc[wb][:, 2:4, :], w_down_v[:, 4 * c + 2:4 * c + 4, :])], 'wd%d' % wb, [], ['WD%d' % wb])
        for j in range(4):
            for t4 in range(4):
                ub = ubc[0] % 4
                ubc[0] += 1
                for kc in range(8):
                    mm(bank(ub), WUc[wb][:, kc, j * 128:(j + 1) * 128], hmT[:, kc, t4 * 512:(t4 + 1) * 512], kc == 0, kc == 7,
                       ['WU%d' % wb, 'hmT'], [bk(ub)])
                act(Rr[ub % 2], bank(ub), AF.Relu, [bk(ub)], ['Rr%d' % (ub % 2)])
                act(uTc[wb][:, j, t4 * 512:(t4 + 1) * 512], Rr[ub % 2], AF.Square, ['Rr%d' % (ub % 2)],
                    ['uT%d' % wb] + (['XTO2', 'XTO3', 'SCRx', 'SCR2x'] if c < 2 and j == 0 and t4 == 0 else []))
        for blk in range(16):
            for nh in range(2):
                yb = 4 + ybc[0] % 4
                ybc[0] += 1
                for j in range(4):
                    mm(bank(yb), uTc[wb][:, j, blk * 128:(blk + 1) * 128], WDc[wb][:, j, nh * 512:(nh + 1) * 512],
                       j == 0, j == 3, ['uT%d' % wb, 'WD%d' % wb], [bk(yb)])
                tt('dve', Yb[blk][:, nh * 512:(nh + 1) * 512], bank(yb), Yb[blk][:, nh * 512:(nh + 1) * 512], ALU.add,
                   [bk(yb), 'Y%d' % blk], ['Y%d' % blk])
    for j in range(16):
        dma('sp', (outd[j * 128:(j + 1) * 128, :], Yb[j]), 'out%d' % (j % 4), ['Y%d' % j], ['outd%d' % (j % 4)])
    S.add('sp', lambda e: None, ['outd%d' % tb for tb in range(4)], [])
    esO.close()
    return nc, es, S


def _host_consts():
    bf = ml_dtypes.bfloat16
    ident = np.eye(128, dtype=np.float32).astype(bf)
    j = np.arange(128)[:, None]
    s = np.arange(128)[None, :]
    negU = np.where(j >= s, -1.0, 0.0).astype(np.float32).astype(bf)
    negO = np.full((128, 128), -1.0, np.float32).astype(bf)
    cmask = np.where(j < s, 0.0, NEG).astype(np.float32).astype(bf)
    h = np.arange(4, dtype=np.float32)
    log_gamma = np.log1p(-(2.0 ** (-5.0 - h))).astype(np.float32)
    idx = np.arange(128, dtype=np.float32)
    scale = np.float32(128.0 ** -0.5)
    rel = idx[None, :] - idx[:, None]
    dec = np.zeros((128, 4, 128), np.float32)
    for hh in range(4):
        dec[:, hh, :] = np.where(rel >= 0, np.exp(log_gamma[hh] * np.where(rel >= 0, rel, 0.0)), 0.0) * scale
    qd = np.zeros((128, 4, 128), np.float32)
    for hh in range(4):
        qd[:, hh, :] = np.exp(log_gamma[hh] * (idx + 1.0))[None, :]
    kd = np.zeros((128, 4, 128), np.float32)
    for hh in range(4):
        kd[:, hh, :] = (np.exp(log_gamma[hh] * (127.0 - idx)) * scale)[:, None]
    invf = (10000.0 ** (-np.arange(0, 128, 2, dtype=np.float32) / 128)).astype(np.float32)
    invf = np.broadcast_to(invf[None, :], (128, 64)).copy()
    return dict(ident=ident, negU=negU, negO=negO, cmask=cmask,
                decayT=dec.reshape(128, 512), qdecT=qd.reshape(128, 512), kdec=kd.reshape(128, 512), invf=invf)


def kernel(x, mem, positions, g_mix, w_in, ret_gn_g, sb_q_g, sb_k_g, w_out, g_xattn, g_mem,
           w_xq, w_xkv, xq_g, xk_g, w_xo, g_mlp, w_up, w_down):
    x = np.asarray(x, np.float32)
    mem = np.asarray(mem, np.float32)
    positions = np.asarray(positions, np.int32)
    consts = _host_consts()

    def col(v):
        v = np.asarray(v, np.float32).reshape(-1, 128)
        return np.ascontiguousarray(v.T)

    gcols = np.concatenate([col(g_mix[0]), col(g_xattn[0]), col(g_mlp[0]), col(g_mem[0]),
                            col(xq_g[0]), col(xk_g[0]), col(ret_gn_g[0]), col(sb_q_g[0]), col(sb_k_g[0])],
                           axis=1).astype(np.float32)
    shared = dict(w_in=np.ascontiguousarray(w_in[0], np.float32), w_out=np.ascontiguousarray(w_out[0], np.float32),
                  w_xq=np.ascontiguousarray(w_xq[0], np.float32), w_xkv=np.ascontiguousarray(w_xkv[0], np.float32),
                  w_xo=np.ascontiguousarray(w_xo[0], np.float32), w_up=np.ascontiguousarray(w_up[0], np.float32),
                  w_down=np.ascontiguousarray(w_down[0], np.float32), gcols=gcols, **consts)
    in_maps = []
    for b in range(4):
        for g in range(2):
            if g == 0:
                xvv = np.concatenate([np.zeros((512, D), np.float32), x[b, :3584]], axis=0)
                pv = np.concatenate([np.zeros((512,), np.int32), positions[b, :3584]], axis=0)
                db = np.full((128, 1), NEG, np.float32)
            else:
                xvv = x[b]
                pv = positions[b]
                db = np.zeros((128, 1), np.float32)
            m = dict(shared)
            m.update(xv=np.ascontiguousarray(xvv), pos=np.ascontiguousarray(pv.reshape(NB, 128).T.astype(np.int32)),
                     dbias=db, mem=np.ascontiguousarray(mem[b]))
            in_maps.append(m)
    nc, es, S = build_program()
    with es:
        S.emit(nc, es)
    res = run_bass_kernel_spmd(nc, in_maps, core_ids=list(range(8)))
    out = np.zeros((4, SEQ, D), np.float32)
    for b in range(4):
        for g in range(2):
            o = res.results[2 * b + g]["out"]
            for ot in range(4):
                t = 2 * ot + g
                out[b, t * 512:(t + 1) * 512] = o[ot * 512:(ot + 1) * 512]
    return out
```

```python
import numpy as np
import ml_dtypes
from contextlib import ExitStack
import concourse.bass as bass
import concourse.mybir as mybir
from concourse.bass_utils import run_bass_kernel_spmd

F32 = mybir.dt.float32
BF16 = mybir.dt.bfloat16
I32 = mybir.dt.int32
AF = mybir.ActivationFunctionType
ALU = mybir.AluOpType
AX = mybir.AxisListType

D = 1024
SEQ = 4096
NB = 32
EPS = 1e-6
NEG = -30000.0
ENGS = ['pe', 'act', 'dve', 'pool', 'sp']


class Op:
    __slots__ = ('eng', 'fn', 'idx', 'waits', 'signal', 'sigval', 'dma', 'dma_val')


class Sched:
    def __init__(self):
        self.ops = {e: [] for e in ENGS}
        self.lastw = {}
        self.readers = {}
        self.known = {e: {} for e in ENGS}
        self.slots = {}
        self.const_keys = []
        self.synced = set()

    def add(self, eng, fn, reads=(), writes=(), slot=None, ndma=1):
        op = Op()
        op.eng = eng
        op.fn = fn
        op.idx = len(self.ops[eng])
        op.waits = []
        op.signal = False
        op.sigval = 0
        op.dma = slot
        op.dma_val = 0
        reads = list(reads)
        writes = list(writes)
        if eng not in self.synced and self.const_keys and slot is None:
            self.synced.add(eng)
            reads = reads + self.const_keys
        if slot is not None:
            self.slots[slot] = self.slots.get(slot, 0) + 16 * ndma
            op.dma_val = self.slots[slot]
        deps = []
        for k in reads:
            w = self.lastw.get(k)
            if w is not None:
                deps.append((w, 'raw'))
        for k in writes:
            w = self.lastw.get(k)
            if w is not None:
                deps.append((w, 'waw'))
            for r in self.readers.get(k, {}).values():
                deps.append((r, 'war'))
        best = {}
        for d, kind in deps:
            if d is op:
                continue
            if d.dma is None:
                if d.eng == eng:
                    if eng == 'pe':
                        continue
                src = d.eng
                val = d.idx
            else:
                src = ('dma', d.dma)
                val = d.dma_val
            if val <= self.known[eng].get(src, -1):
                continue
            if src not in best or val > best[src][0]:
                best[src] = (val, d)
        for src, (val, d) in best.items():
            self.known[eng][src] = val
            if d.dma is None:
                d.signal = True
            op.waits.append(d)
        for k in reads:
            self.readers.setdefault(k, {})[(eng, slot)] = op
        for k in writes:
            self.lastw[k] = op
            self.readers[k] = {}
        self.ops[eng].append(op)
        return op

    def emit(self, nc, es):
        for e in ENGS:
            c = 0
            for op in self.ops[e]:
                if op.signal:
                    c += 1
                    op.sigval = c
        sems = {e: es.enter_context(nc.semaphore('s_' + e)) for e in ENGS}
        dsem = {s: es.enter_context(nc.semaphore('d_%d' % i)) for i, s in enumerate(self.slots)}

        def run(e, eng):
            for op in self.ops[e]:
                for d in op.waits:
                    if d.dma is None:
                        eng.wait_ge(sems[d.eng], d.sigval)
                    else:
                        eng.wait_ge(dsem[d.dma], d.dma_val)
                ins = op.fn(eng)
                if ins is None:
                    continue
                if op.dma is not None:
                    if not isinstance(ins, (list, tuple)):
                        ins = [ins]
                    for i in ins:
                        i.then_inc(dsem[op.dma], 16)
                elif op.signal:
                    ins.then_inc(sems[e], 1)

        with nc.Block() as block:
            @block.tensor
            def _(eng):
                run('pe', eng)

            @block.scalar
            def _(eng):
                run('act', eng)

            @block.vector
            def _(eng):
                run('dve', eng)

            @block.gpsimd
            def _(eng):
                run('pool', eng)

            @block.sync
            def _(eng):
                run('sp', eng)


def build_program(stop=None):
    nc = bass.Bass("TRN2", target_bir_lowering=False)
    es = ExitStack()
    S = Sched()
    dbgd = nc.dram_tensor("dbg", [128, 16384], F32, kind="ExternalOutput").ap() if stop else None

    def finish(dumps):
        for i, (ap, key, c0, n) in enumerate(dumps):
            if ap.dtype != F32:
                S.add('dve', lambda e, ap=ap, c0=c0, n=n: e.tensor_copy(out=dbgs[:, c0:c0 + n], in_=ap), [key], ['dbgs%d' % i])
                S.add('sp', lambda e, c0=c0, n=n: [e.dma_start(out=dbgd[:, c0:c0 + n], in_=dbgs[:, c0:c0 + n])], ['dbgs%d' % i], ['dbgout'], slot='dbg%d' % i)
            else:
                S.add('sp', lambda e, ap=ap, c0=c0, n=n: [e.dma_start(out=dbgd[:, c0:c0 + n], in_=ap)], [key], ['dbgout'], slot='dbg%d' % i)
        S.add('sp', lambda e: None, ['dbgout'], [])
        return nc, es, S

    def din(name, shape, dt):
        return nc.dram_tensor(name, list(shape), dt, kind="ExternalInput").ap()

    xv = din("xv", [SEQ, D], F32)
    posd = din("pos", [128, NB], I32)
    dbiasd = din("dbias", [128, 1], F32)
    memd = din("mem", [256, D], F32)
    w_in = din("w_in", [D, 3584], F32)
    w_out = din("w_out", [D, D], F32)
    w_xq = din("w_xq", [D, D], F32)
    w_xkv = din("w_xkv", [D, 2 * D], F32)
    w_xo = din("w_xo", [D, D], F32)
    w_up = din("w_up", [D, 4 * D], F32)
    w_down = din("w_down", [4 * D, D], F32)
    gcols_d = din("gcols", [128, 60], F32)
    identd = din("ident", [128, 128], BF16)
    negUd = din("negU", [128, 128], BF16)
    negOd = din("negO", [128, 128], BF16)
    cmaskd = din("cmask", [128, 128], BF16)
    decd = din("decayT", [128, 512], F32)
    qdecd = din("qdecT", [128, 512], F32)
    kdecd = din("kdec", [128, 512], F32)
    invfd = din("invf", [128, 64], F32)
    outd = nc.dram_tensor("out", [2048, D], F32, kind="ExternalOutput").ap()

    def sb(name, shape, dt, stack=None):
        return (stack or es).enter_context(nc.sbuf_tensor('s_' + name, list(shape), dt))

    ps = es.enter_context(nc.psum_tensor("ps", [128, 8, 512], F32))
    dbgs = sb("dbgs", [128, 4096], F32) if stop else None

    def bank(b):
        return ps[:, b, :]

    def bankbf(b):
        return ps[:, b, :].bitcast(BF16)

    def bk(b):
        return 'B%d' % b

    WB = sb("WB", [128, 28672], BF16)
    KV = sb("KV", [128, 32768], BF16)
    mixT = sb("mixT", [128, 8, 2048], BF16)
    XT = [sb("XT%d" % i, [128, 1024], F32) for i in range(2)]
    HB = [sb("HB%d" % i, [128, 1024], BF16) for i in range(2)]
    FT = [sb("FT%d" % i, [128, 8, 128], BF16) for i in range(2)]
    SCR = sb("SCR", [128, 1024], F32)
    SCR2 = sb("SCR2", [128, 1024], F32)
    st = sb("st", [128, 256], F32)
    SL = {"s": 0}
    ident = sb("ident", [128, 128], BF16)
    negU = sb("negU", [128, 128], BF16)
    negO = sb("negO", [128, 128], BF16)
    cmask = sb("cmask", [128, 128], BF16)
    zeros = sb("zeros", [128, 512], BF16)
    gcols = sb("gcols", [128, 60], F32)
    dbias = sb("dbias", [128, 1], F32)
    invf = sb("invf", [128, 64], F32)
    posi = sb("posi", [128, NB], I32)
    posf = sb("posf", [128, NB], F32)
    epsb = sb("epsb", [128, 1], F32)
    oneb = sb("oneb", [128, 1], F32)
    xkT = sb("xkT", [128, 8, 256], BF16)
    xvb = sb("xvb", [128, 2, 1024], BF16)
    G_MIX, G_XA, G_MLP, G_MEM, G_XQ, G_XK, G_GN, G_SQ, G_SK = 0, 8, 16, 24, 32, 40, 48, 52, 56

    def dma(eng, pairs, slot, reads=(), writes=()):
        if not isinstance(pairs, list):
            pairs = [pairs]
        S.add(eng, lambda e: [e.dma_start(out=o, in_=i) for (o, i) in pairs], reads, writes, slot=slot, ndma=len(pairs))

    def mm(out, lhsT, rhs, start, stop, reads, writes, skip=False):
        S.add('pe', lambda e: e.matmul(out, lhsT, rhs, start=start, stop=stop, skip_group_check=skip), reads, writes)

    def tr(out, in_, reads, writes):
        S.add('pe', lambda e: e.transpose(out, in_, ident[:]), list(reads) + ['ident'], writes)

    def act(out, in_, func, reads, writes, bias=None, scale=None, accum_out=None):
        kw = {}
        if bias is not None:
            kw['bias'] = bias
        if scale is not None:
            kw['scale'] = scale
        if accum_out is not None:
            kw['accum_out'] = accum_out
        S.add('act', lambda e: e.activation(out=out, in_=in_, func=func, **kw), reads, writes)

    def tt(eng, out, in0, in1, op, reads, writes):
        S.add(eng, lambda e: e.tensor_tensor(out=out, in0=in0, in1=in1, op=op), reads, writes)

    def ts(eng, out, in0, s1, op0, reads, writes, s2=None, op1=None):
        if op1 is None:
            S.add(eng, lambda e: e.tensor_scalar(out=out, in0=in0, scalar1=s1, scalar2=None, op0=op0), reads, writes)
        else:
            S.add(eng, lambda e: e.tensor_scalar(out=out, in0=in0, scalar1=s1, scalar2=s2, op0=op0, op1=op1), reads, writes)

    def stt(out, in0, scalar, in1, op0, op1, reads, writes):
        S.add('dve', lambda e: e.scalar_tensor_tensor(out=out, in0=in0, scalar=scalar, in1=in1, op0=op0, op1=op1), reads, writes)

    import os
    VAR = os.environ.get('KVAR', '')

    def cp(eng, out, in_, reads, writes):
        if eng == 'act' and in_.dtype == BF16:
            eng = 'dve'
        if eng == 'pool' and 'P' in VAR:
            eng = 'dve'
        if eng == 'act':
            S.add('act', lambda e: e.copy(out=out, in_=in_), reads, writes)
        else:
            S.add(eng, lambda e: e.tensor_copy(out=out, in_=in_), reads, writes)

    def reduce_(out, in_, op, reads, writes):
        S.add('dve', lambda e: e.tensor_reduce(out=out, in_=in_, axis=AX.X, op=op), reads, writes)

    def rstd_from_ss(ss, tmp, out, inv_n, key_ss, key_tmp, key_out):
        act(tmp, ss, AF.Ln, [key_ss], [key_tmp], bias=EPS, scale=inv_n)
        act(out, tmp, AF.Exp, [key_tmp], [key_out], scale=-0.5)

    nbar = [0]

    def barrier():
        n = nbar[0]
        nbar[0] += 1
        keys = ['bar%d_%s' % (n, e) for e in ('pe', 'act', 'dve', 'pool')]
        S.add('pe', lambda e: e.matmul(bank(7)[:, 0:1], ident[:], zeros[:, 0:1], start=True, stop=True),
              ['ident', 'zeros'], [keys[0], bk(7)])
        S.add('act', lambda e: e.copy(out=st[:, 121:122], in_=st[:, 120:121]), [], [keys[1]])
        S.add('dve', lambda e: e.tensor_copy(out=st[:, 123:124], in_=st[:, 122:123]), [], [keys[2]])
        S.add('pool', lambda e: e.memset(st[:, 124:125], 0.0), [], [keys[3]])
        for e_ in ('pe', 'act', 'dve', 'pool', 'sp'):
            S.add(e_, lambda e: None, keys, [])

    ck = []

    def cload(dst, src, key):
        dma('sp', (dst, src), 'const', [], [key])
        ck.append(key)

    cload(ident[:], identd, 'ident')
    cload(negU[:], negUd, 'negU')
    cload(negO[:], negOd, 'negO')
    cload(cmask[:], cmaskd, 'cmask')
    cload(gcols[:], gcols_d, 'gcols')
    cload(dbias[:], dbiasd, 'dbias')
    cload(invf[:], invfd, 'invf')
    cload(posi[:], posd, 'posi')
    S.add('pool', lambda e: e.memset(zeros[:], 0.0), [], ['zeros'])
    S.add('pool', lambda e: e.memset(epsb[:], EPS), [], ['epsb'])
    S.add('pool', lambda e: e.memset(oneb[:], 1.0), [], ['oneb'])
    S.add('pool', lambda e: e.memset(st[:], 0.0), [], ['st0'])
    S.const_keys = list(dict.fromkeys(ck)) + ['zeros', 'epsb', 'oneb', 'st0']

    WR = WB[:, 0:16384].rearrange("p (k n) -> p k n", k=8)
    WS = WB[:, 16384:28672].rearrange("p (k n) -> p k n", k=8)
    WO = WB[:, 0:8192].rearrange("p (k n) -> p k n", k=8)
    WQ = WB[:, 8192:16384].rearrange("p (k n) -> p k n", k=8)
    WX = WB[:, 16384:24576].rearrange("p (k n) -> p k n", k=8)
    RrF = WB[:, 24576:28672].bitcast(F32)
    w_in_v = w_in.rearrange("(k p) n -> p k n", p=128)
    w_out_v = w_out.rearrange("(k p) n -> p k n", p=128)
    w_xq_v = w_xq.rearrange("(k p) n -> p k n", p=128)
    w_xo_v = w_xo.rearrange("(k p) n -> p k n", p=128)
    w_xkv_v = w_xkv.rearrange("(k p) n -> p k n", p=128)
    w_up_v = w_up.rearrange("(k p) n -> p k n", p=128)
    w_down_v = w_down.rearrange("(f p) n -> p f n", p=128)
    WKV = KV[:, 0:16384].rearrange("p (k n) -> p k n", k=8)
    dma('pool', [(WKV[:, kc, :], w_xkv_v[:, kc, :]) for kc in range(8)], 'wkv', [], ['KVa'])
    dma('pool', [(WR[:, kc, :], w_in_v[:, kc, 0:2048]) for kc in range(8)], 'wr', [], ['WBa'])
    dma('pool', [(WS[:, kc, :], w_in_v[:, kc, 2048:3584]) for kc in range(8)], 'ws', [], ['WBc'])

    if stop == 'const':
        return finish([(gcols[:], 'gcols', 0, 60), (WR[:, 0, 0:512], 'WBa', 64, 512)])
    cnt = {'xt': 0, 'hb': 0, 'ft': 0}

    def next_hb():
        i = cnt['hb'] % 2
        cnt['hb'] += 1
        return HB[i], 'HB%d' % i

    def next_ft():
        i = cnt['ft'] % 2
        cnt['ft'] += 1
        return FT[i], 'FT%d' % i

    def transpose_gain(srcb, src_key, nk, gc, dstT, dst_key):
        pb_ = 7 - SL['s']
        ptb = bankbf(pb_)
        for k in range(nk):
            tr(ptb[:, k * 128:(k + 1) * 128], srcb[:, k * 128:(k + 1) * 128], [src_key], [bk(pb_)])
        pv = ptb[:, 0:nk * 128].rearrange("p (k t) -> p k t", k=nk)
        if gc is None:
            cp('act', dstT, pv, [bk(pb_)], [dst_key])
        else:
            tt('dve', dstT, pv, gc.unsqueeze(2).to_broadcast([128, nk, 128]), ALU.mult, [bk(pb_), 'gcols'], [dst_key])

    def norm_T(src_ap, src_key, gcol0, dstT, dst_key):
        hb, hk = next_hb()
        so = 128 * SL['s']
        sx = '_%d' % SL['s']
        act(hb[:], src_ap, AF.Square, [src_key], [hk, 'st_ss' + sx], accum_out=st[:, so:so + 1])
        rstd_from_ss(st[:, so:so + 1], st[:, so + 1:so + 2], st[:, so + 2:so + 3], 1.0 / 1024, 'st_ss' + sx, 'st_ln' + sx, 'st_rs' + sx)
        if SL.get('actnorm'):
            act(hb[:], src_ap, AF.Copy, [src_key, 'st_rs' + sx], [hk], scale=st[:, so + 2:so + 3])
        else:
            ts('dve', hb[:], src_ap, st[:, so + 2:so + 3], ALU.mult, [src_key, 'st_rs' + sx], [hk])
        transpose_gain(hb, hk, 8, gcols[:, gcol0:gcol0 + 8], dstT, dst_key)

    def front(blk):
        i = cnt['xt'] % 2
        cnt['xt'] += 1
        dma('sp', (XT[i][:], xv[blk * 128:(blk + 1) * 128, :]), 'xt%d' % i, [], ['XT%d' % i])
        ft, fk = next_ft()
        norm_T(XT[i][:], 'XT%d' % i, G_MIX, ft[:], fk)
        return ft, fk

    def proj(hT, hkey, W, wkeys, c0, b):
        for kc in range(8):
            mm(bank(b), hT[:, kc, :], W[:, kc, c0:c0 + 512], kc == 0, kc == 7, [hkey] + wkeys, [bk(b)])

    def is_own(blk):
        return (blk // 4) % 2 == 1

    def own_col(blk):
        return (blk // 8) * 512 + (blk % 4) * 128

    KVf = KV[:, :].bitcast(F32)
    COS = KVf[:, 8192:10240].rearrange("p (b d) -> p b d", b=NB)
    SIN = KVf[:, 10240:12288].rearrange("p (b d) -> p b d", b=NB)
    decT = KVf[:, 12288:12800]
    qdecT = KVf[:, 12800:13312]
    kdec = KVf[:, 13312:13824]
    Sf = KVf[:, 13824:14336]
    KVb16 = KV[:, 28672:32768]
    Sb = KVb16[:, 0:512]
    rqb = KVb16[:, 512:1024]
    rkb = KVb16[:, 1024:1536]
    rkd = KVb16[:, 1536:2048]
    rvb = KVb16[:, 2048:2560]
    rqT = KVb16[:, 2560:3072]
    rqTd = KVb16[:, 3072:3584]
    rkT = KVb16[:, 3584:4096]
    sTm = HB[0]

    SCRS = [(SCR[:], SCR2[:], ['SCRa', 'SCRb'], ['SCR2a', 'SCR2b']),
            (KVf[:, 10240:11264], KVf[:, 11264:12288], ['SCRx'], ['SCR2x'])]

    def head4_norm(b0, dstb, dkey):
        s_ = SL['s']
        A_, B_, ka, kb_ = SCRS[s_]
        so = 128 * s_
        sx = '_%d' % s_
        cp('act', A_[:, 0:512], bank(b0), [bk(b0)], ka)
        cp('act', A_[:, 512:1024], bank(b0 + 1), [bk(b0 + 1)], ka)
        tt('dve', B_, A_, A_, ALU.mult, ka, kb_)
        reduce_(st[:, so + 48:so + 52], B_.rearrange("p (h d) -> p h d", h=4), ALU.add, kb_, ['st_h' + sx])
        rstd_from_ss(st[:, so + 48:so + 52], st[:, so + 52:so + 56], st[:, so + 56:so + 60], 1.0 / 256, 'st_h' + sx, 'st_hl' + sx, 'st_hr' + sx)
        tt('dve', dstb.rearrange("p (h d) -> p h d", h=4), A_.rearrange("p (h d) -> p h d", h=4),
           st[:, so + 56:so + 60].unsqueeze(2).to_broadcast([128, 4, 256]), ALU.mult, ka + ['st_hr' + sx], [dkey])

    def proj2(hT, hkey, W, wkeys, c0, b0):
        for nh in range(2):
            for kc in range(8):
                mm(bank(b0 + nh), hT[:, kc, :], W[:, kc, c0 + nh * 512:c0 + (nh + 1) * 512], kc == 0, kc == 7,
                   [hkey] + wkeys, [bk(b0 + nh)])

    for mb in range(2):
        i = cnt['xt'] % 2
        cnt['xt'] += 1
        dma('sp', (XT[i][:], memd[mb * 128:(mb + 1) * 128, :]), 'xt%d' % i, [], ['XT%d' % i])
        ft, fk = next_ft()
        norm_T(XT[i][:], 'XT%d' % i, G_MEM, ft[:], fk)
        proj2(ft, fk, WKV, ['KVa'], 0, 0)
        hb, hk = next_hb()
        head4_norm(0, hb[:], hk)
        transpose_gain(hb, hk, 8, gcols[:, G_XK:G_XK + 8], xkT[:, :, mb * 128:(mb + 1) * 128], 'xkT')
        proj2(ft, fk, WKV, ['KVa'], 1024, 2)
        cp('act', xvb[:, mb, 0:512], bank(2), [bk(2)], ['xvb'])
        cp('act', xvb[:, mb, 512:1024], bank(3), [bk(3)], ['xvb'])
    barrier()
    if stop == 'memkv':
        return finish([(xkT[:].rearrange('p k m -> p (k m)'), 'xkT', 0, 2048), (xvb[:].rearrange('p k m -> p (k m)'), 'xvb', 2048, 2048)])

    TWO_PI = 2.0 * np.pi
    cp('dve', posf[:], posi[:], ['posi'], ['posf'])
    KI = XT[0][:].bitcast(I32)
    for half in range(2):
        bs = slice(half * 16, half * 16 + 16)
        ANG = SCR[:].rearrange("p (b d) -> p b d", b=16)
        A2 = SCR2[:].rearrange("p (b d) -> p b d", b=16)
        KIv = KI.rearrange("p (b d) -> p b d", b=16)
        tt('dve', ANG, posf[:, bs].unsqueeze(2).to_broadcast([128, 16, 64]),
           invf[:, :].unsqueeze(1).to_broadcast([128, 16, 64]), ALU.mult, ['posf', 'invf'], ['ANG'])
        for (dst, shift, key) in ((SIN[:, bs, :], 0.0, 'SIN'), (COS[:, bs, :], float(np.pi / 2), 'COS')):
            ts('dve', A2, ANG, shift, ALU.add, ['ANG'], ['A2'], s2=1.0 / TWO_PI, op1=ALU.mult)
            cp('dve', KIv, A2, ['A2'], ['KI'])
            cp('dve', A2, KIv, ['KI'], ['A2'])
            c1 = 6.28125
            c2 = float(TWO_PI - 6.28125)
            stt(dst, A2, -c1, ANG, ALU.mult, ALU.add, ['A2', 'ANG'], [key])
            stt(dst, A2, -c2, dst, ALU.mult, ALU.add, ['A2', key], [key])
            if shift != 0.0:
                ts('dve', dst, dst, shift, ALU.add, [key], [key])
            ts('dve', A2, dst, float(np.pi), ALU.is_gt, [key], ['A2'], s2=-TWO_PI, op1=ALU.mult)
            tt('dve', dst, dst, A2, ALU.add, [key, 'A2'], [key])
            ts('dve', A2, dst, float(-np.pi), ALU.is_lt, [key], ['A2'], s2=TWO_PI, op1=ALU.mult)
            tt('dve', dst, dst, A2, ALU.add, [key, 'A2'], [key])
            act(dst, dst, AF.Sin, [key], [key])
    dma('sp', [(decT, decd), (qdecT, qdecd), (kdec, kdecd)], 'rconst', [], ['decT', 'qdecT', 'kdec'])
    S.add('pool', lambda e: e.memset(Sf, 0.0), [], ['Sf'])
    S.add('pool', lambda e: e.memset(Sb, 0.0), [], ['Sb'])
    barrier()
    if stop == 'rot':
        return finish([(COS.rearrange('p b d -> p (b d)'), 'COS', 0, 2048), (SIN.rearrange('p b d -> p (b d)'), 'SIN', 2048, 2048)])

    SL['actnorm'] = True
    CDEC = [float(np.exp(np.float32(np.log1p(np.float32(-(2.0 ** (-5.0 - h))))) * np.float32(128.0))) for h in range(4)]
    sTm = KV[:, 0:512]
    rob = KV[:, 512:1024]

    def rotary(b, blk, dst, dkey):
        src = bank(b).rearrange("p (h two d) -> p h two d", h=4, two=2)
        dv = dst.rearrange("p (h two d) -> p h two d", h=4, two=2)
        cb = COS[:, blk, :].unsqueeze(1).to_broadcast([128, 4, 64])
        sbb = SIN[:, blk, :].unsqueeze(1).to_broadcast([128, 4, 64])
        t = SCR[:].rearrange("p (a h d) -> p a h d", a=4, h=4)
        tt('dve', t[:, 0], src[:, :, 0, :], cb, ALU.mult, [bk(b), 'COS'], ['SCRa'])
        tt('dve', t[:, 1], src[:, :, 1, :], sbb, ALU.mult, [bk(b), 'SIN'], ['SCRa'])
        tt('dve', t[:, 2], src[:, :, 1, :], cb, ALU.mult, [bk(b), 'COS'], ['SCRb'])
        tt('dve', t[:, 3], src[:, :, 0, :], sbb, ALU.mult, [bk(b), 'SIN'], ['SCRb'])
        tt('dve', dv[:, :, 0, :], t[:, 0], t[:, 1], ALU.subtract, ['SCRa'], [dkey])
        tt('dve', dv[:, :, 1, :], t[:, 2], t[:, 3], ALU.add, ['SCRb'], [dkey])

    nxt_front = front(0)
    for blk in range(NB):
        own = is_own(blk)
        hT, hk = nxt_front
        if blk + 1 < NB:
            nxt_front = front(blk + 1)
        wk = ['WBa']
        if own:
            proj(hT, hk, WR, wk, 0, 0)
            rotary(0, blk, rqb, 'rqb')
            if stop == 'R%d%s' % (blk, 'a'):
                return finish([(Sf, 'Sf', 0, 512)])
        proj(hT, hk, WR, wk, 512, 1)
        rotary(1, blk, SCR2[:, 0:512], 'SCR2a')
        tt('pool', rkd, SCR2[:, 0:512], kdec, ALU.mult, ['SCR2a', 'kdec'], ['rkd'])
        if own:
            cp('pool', rkb, SCR2[:, 0:512], ['SCR2a'], ['rkb'])
        proj(hT, hk, WR, wk, 1024, 2)
        cp('act', rvb, bank(2), [bk(2)], ['rvb'])
        if own:
            proj(hT, hk, WR, wk, 1536, 3)
            act(SCR2[:, 512:1024], bank(3), AF.Silu, [bk(3)], ['SCR2b'])
            ptb = bankbf(7)
            for h in range(4):
                tr(ptb[:, h * 128:(h + 1) * 128], rqb[:, h * 128:(h + 1) * 128], ['rqb'], [bk(7)])
            for h in range(4):
                tr(ptb[:, 512 + h * 128:512 + (h + 1) * 128], rkb[:, h * 128:(h + 1) * 128], ['rkb'], [bk(7)])
            cp('act', rqT, ptb[:, 0:512], [bk(7)], ['rqT'])
            tt('dve', rqTd, ptb[:, 0:512], qdecT, ALU.mult, [bk(7), 'qdecT'], ['rqTd'])
            cp('act', rkT, ptb[:, 512:1024], [bk(7)], ['rkT'])
            if stop == 'R%d%s' % (blk, 'b'):
                return finish([(Sf, 'Sf', 0, 512)])
            for h in range(4):
                hs = slice(h * 128, (h + 1) * 128)
                mm(bank(4)[:, hs], rkT[:, hs], rqT[:, hs], True, True, ['rkT', 'rqT'], [bk(4)])
            tt('dve', sTm, bank(4), decT, ALU.mult, [bk(4), 'decT'], ['sTm'])
            if stop == 'R%d%s' % (blk, 'c'):
                return finish([(Sf, 'Sf', 0, 512)])
            for h in range(4):
                hs = slice(h * 128, (h + 1) * 128)
                mm(bank(5)[:, hs], sTm[:, hs], rvb[:, hs], True, False, ['sTm', 'rvb'], [bk(5)])
                mm(bank(5)[:, hs], rqTd[:, hs], Sb[:, hs], False, True, ['rqTd', 'Sb'], [bk(5)])
            cp('act', SCR[:, 0:512], bank(5), [bk(5)], ['SCRa'])
            tt('dve', SCR[:, 512:1024], SCR[:, 0:512], SCR[:, 0:512], ALU.mult, ['SCRa'], ['SCRb'])
            reduce_(st[:, 8:12], SCR[:, 512:1024].rearrange("p (h d) -> p h d", h=4), ALU.add, ['SCRb'], ['st_g'])
            rstd_from_ss(st[:, 8:12], st[:, 12:16], st[:, 16:20], 1.0 / 128, 'st_g', 'st_gl', 'st_gr')
            if stop == 'R%d%s' % (blk, 'd'):
                return finish([(Sf, 'Sf', 0, 512)])
            for h in range(4):
                hs = slice(h * 128, (h + 1) * 128)
                stt(rob[:, hs], SCR[:, hs], st[:, 16 + h:17 + h], SCR2[:, 512 + h * 128:512 + (h + 1) * 128],
                    ALU.mult, ALU.mult, ['SCRa', 'st_gr', 'SCR2b'], ['rob'])
            c0 = own_col(blk)
            transpose_gain(rob, 'rob', 4, gcols[:, G_GN:G_GN + 4], mixT[:, 0:4, c0:c0 + 128], 'mixT')
            if stop == 'R%d%s' % (blk, 'e'):
                return finish([(Sf, 'Sf', 0, 512)])
        for h in range(4):
            hs = slice(h * 128, (h + 1) * 128)
            mm(bank(6)[:, hs], rkd[:, hs], rvb[:, hs], True, True, ['rkd', 'rvb'], [bk(6)])
        for h in range(4):
            hs = slice(h * 128, (h + 1) * 128)
            stt(Sf[:, hs], Sf[:, hs], CDEC[h], bank(6)[:, hs], ALU.mult, ALU.add, ['Sf', bk(6)], ['Sf'])
        cp('pool', Sb, Sf, ['Sf'], ['Sb'])
        if stop == 'R%d' % blk:
            return finish([(Sf, 'Sf', 2048, 512), (rkd, 'rkd', 2560, 512), (rvb, 'rvb', 3072, 512)])
    barrier()
    if stop == 'R':
        return finish([(mixT[:, k, 0:1024], 'mixT', k * 1024, 1024) for k in range(4)])
    dma('pool', [(WO[:, kc, :], w_out_v[:, kc, :]) for kc in range(8)], 'wo', [], ['WBa'])
    dma('pool', [(WQ[:, kc, :], w_xq_v[:, kc, :]) for kc in range(8)], 'wq', [], ['WBa'])

    SL['actnorm'] = False
    esS = ExitStack()
    skT = KV[:, 0:16384].rearrange("p (g t) -> p g t", g=4)
    svv = KV[:, 16384:32768].rearrange("p (b c) -> p b c", b=NB)
    sqT = sb("sqT", [128, 4, 512], BF16, esS)
    Ebuf = [SCR[:, 0:512], SCR[:, 512:1024]]
    Ekey = ['SCRa', 'SCRb']
    Lb = [[sb("Lb%d%d" % (i, j), [128, 512], BF16, esS) for j in range(2)] for i in range(2)]
    wT = [[sb("wT%d%d" % (i, j), [128, 512], BF16, esS) for j in range(2)] for i in range(2)]
    Laccb = [[sb("Laccb%d%d" % (i, j), [128, 512], BF16, esS) for j in range(2)] for i in range(2)]
    Lacc = [SCR2[:, 0:512], SCR2[:, 512:1024]]
    Lkey = ['SCR2a', 'SCR2b']

    def qk_norm(b, scale, dstb, dkey):
        cp('act', SCR2[:, 0:512], bank(b), [bk(b)], ['SCR2a'])
        tt('dve', SCR2[:, 512:1024], SCR2[:, 0:512], SCR2[:, 0:512], ALU.mult, ['SCR2a'], ['SCR2b'])
        reduce_(st[:, 24:32], SCR2[:, 512:1024].rearrange("p (h d) -> p h d", h=8), ALU.add, ['SCR2b'], ['st_q'])
        rstd_from_ss(st[:, 24:32], st[:, 32:40], st[:, 40:48], 1.0 / 64, 'st_q', 'st_ql', 'st_qr')
        if scale != 1.0:
            ts('dve', st[:, 40:48], st[:, 40:48], scale, ALU.mult, ['st_qr'], ['st_qr'])
        tt('dve', dstb.rearrange("p (h d) -> p h d", h=8), SCR2[:, 0:512].rearrange("p (h d) -> p h d", h=8),
           st[:, 40:48].unsqueeze(2).to_broadcast([128, 8, 64]), ALU.mult, ['SCR2a', 'st_qr'], [dkey])

    nxt_front = front(0)
    for blk in range(NB):
        own = is_own(blk)
        hT, hk = nxt_front
        if blk + 1 < NB:
            nxt_front = front(blk + 1)
        wk = ['WBc']
        proj(hT, hk, WS, wk, 512, 1)
        hb, hbk = next_hb()
        qk_norm(1, 1.0, hb[:, 0:512], hbk)
        transpose_gain(hb, hbk, 4, gcols[:, G_SK:G_SK + 4], skT[:, :, blk * 128:(blk + 1) * 128], 'skT')
        proj(hT, hk, WS, wk, 1024, 2)
        cp('act', svv[:, blk, :], bank(2), [bk(2)], ['sv'])
        if own:
            proj(hT, hk, WS, wk, 0, 0)
            hb, hbk = next_hb()
            qk_norm(0, 0.125, hb[:, 0:512], hbk)
            i4 = blk % 4
            transpose_gain(hb, hbk, 4, gcols[:, G_SQ:G_SQ + 4], sqT[:, :, i4 * 128:(i4 + 1) * 128], 'sqT')
        if own and blk % 4 == 3:
            vt = blk // 4
            tcol = (vt // 2) * 512
            nsteps = 4 * vt + 4
            def geo(pr, step, g):
                kb = 4 * vt + 3 - step
                di = kb - 4 * vt
                n0 = 128 * di if di >= 0 else 0
                n0n = 128 * (di - 1) if di - 1 >= 0 else 0
                return kb, di, n0, n0n, slice(n0, 512), g % 2, step == nsteps - 1, kb < 4, g % 3

            def stA(pr, step, g):
                kb, di, n0, n0n, cs, par, last, useb, z3 = geo(pr, step, g)
                for hh in range(2):
                    pl = slice(64 * hh, 64 * hh + 64)
                    zb = hh * 3 + z3
                    mm(bank(zb)[:, cs], skT[pl, pr, kb * 128:(kb + 1) * 128], sqT[pl, pr, cs], True, di < 0,
                       ['skT', 'sqT'], [bk(zb)])
                    if di >= 0:
                        mm(bank(zb)[:, n0:n0 + 128], ident[:], cmask[:], False, True, ['ident', 'cmask'], [bk(zb)])

            def stB(pr, step, g):
                kb, di, n0, n0n, cs, par, last, useb, z3 = geo(pr, step, g)
                for hh in range(2):
                    zb = hh * 3 + z3
                    if useb:
                        act(Ebuf[hh][:, cs], bank(zb)[:, cs], AF.Exp, [bk(zb), 'dbias'], [Ekey[hh]], bias=dbias[:, 0:1])
                    else:
                        act(Ebuf[hh][:, cs], bank(zb)[:, cs], AF.Exp, [bk(zb)], [Ekey[hh]])
                for hh in range(2):
                    act(Lb[hh][par][:, cs], Ebuf[hh][:, cs], AF.Ln, [Ekey[hh]], ['L%d%d' % (hh, par)],
                        bias=1.0)
                if not last:
                    csn = slice(n0n, 512)
                    for hh in range(2):
                        tt('dve', Lacc[hh][:, cs], Lacc[hh][:, cs], Lb[hh][par][:, cs], ALU.add,
                           [Lkey[hh], 'L%d%d' % (hh, par)], [Lkey[hh]])
                    for hh in range(2):
                        cp('dve', Laccb[hh][par][:, csn], Lacc[hh][:, csn], [Lkey[hh]], ['LA%d%d' % (hh, par)])

            def stC(pr, step, g):
                kb, di, n0, n0n, cs, par, last, useb, z3 = geo(pr, step, g)
                for hh in range(2):
                    zb = hh * 3 + z3
                    mm(bank(zb)[:, cs], negU[:], Lb[hh][par][:, cs], False, True,
                       ['negU', 'L%d%d' % (hh, par)], [bk(zb)], skip=True)
                    if step > 0:
                        mm(bank(zb)[:, cs], negO[:], Laccb[hh][1 - par][:, cs], False, True,
                           ['negO', 'LA%d%d' % (hh, 1 - par)], [bk(zb)], skip=True)

            def stD(pr, step, g):
                kb, di, n0, n0n, cs, par, last, useb, z3 = geo(pr, step, g)
                for hh in range(2):
                    zb = hh * 3 + z3
                    if useb:
                        act(wT[hh][par][:, cs], bank(zb)[:, cs], AF.Exp, [bk(zb), 'dbias'], ['w%d%d' % (hh, par)],
                            bias=dbias[:, 0:1])
                    else:
                        act(wT[hh][par][:, cs], bank(zb)[:, cs], AF.Exp, [bk(zb)], ['w%d%d' % (hh, par)])

            def stE(pr, step, g):
                kb, di, n0, n0n, cs, par, last, useb, z3 = geo(pr, step, g)
                for hh in range(2):
                    mm(bank(6 + hh)[:, cs], svv[:, kb, pr * 128:(pr + 1) * 128], wT[hh][par][:, cs], False, True,
                       ['sv', 'w%d%d' % (hh, par)], [bk(6 + hh)], skip=True)


            def pair_init(pr):
                for hh in range(2):
                    mm(bank(6 + hh), negO[:], zeros[:], True, True, ['negO', 'zeros'], [bk(6 + hh)])

            def lacc_init():
                for hh in range(2):
                    S.add('pool', lambda e, hh=hh: e.memset(Lacc[hh], 0.0), [], [Lkey[hh]])

            def pair_evac(pr):
                cp('act', mixT[0:64, 4 + pr, tcol:tcol + 512], bank(6)[0:64, :], [bk(6)], ['mixT'])
                cp('dve', mixT[64:128, 4 + pr, tcol:tcol + 512], bank(7)[64:128, :], [bk(7)], ['mixT'])

            FL = [(pr, step) for pr in range(4) for step in range(nsteps)]
            G = len(FL)
            pair_init(0)
            lacc_init()
            stA(FL[0][0], FL[0][1], 0)
            for g in range(G):
                if g + 1 < G:
                    stA(FL[g + 1][0], FL[g + 1][1], g + 1)
                if g > 0 and FL[g][1] == 0:
                    lacc_init()
                stB(FL[g][0], FL[g][1], g)
                if g >= 1:
                    stD(FL[g - 1][0], FL[g - 1][1], g - 1)
                stC(FL[g][0], FL[g][1], g)
                if g >= 1:
                    stE(FL[g - 1][0], FL[g - 1][1], g - 1)
                    if FL[g - 1][1] == nsteps - 1:
                        pair_evac(FL[g - 1][0])
                        pair_init(FL[g][0])
            stD(FL[G - 1][0], FL[G - 1][1], G - 1)
            stE(FL[G - 1][0], FL[G - 1][1], G - 1)
            pair_evac(FL[G - 1][0])
    barrier()
    if stop == 'S':
        return finish([(mixT[:, 4 + k, 0:1024], 'mixT', k * 1024, 1024) for k in range(4)])
    esS.close()

    SL['actnorm'] = True
    dma('pool', [(WX[:, kc, :], w_xo_v[:, kc, :]) for kc in range(8)], 'wx', [], ['WBc'])
    hmT = KV[:, 0:16384].rearrange("p (k t) -> p k t", k=8)
    esO1 = ExitStack()
    FTO = [sb("FTO%d" % i, [128, 8, 128], BF16, esO1) for i in range(4)]
    XTO = [XT[0][:], XT[1][:], KVf[:, 8192:9216], KVf[:, 9216:10240]]

    def frontO(bo):
        ot, i4 = bo // 4, bo % 4
        row = (2 * ot + 1) * 4 + i4
        tc0 = bo * 128
        xi = bo % 4
        x1 = XTO[xi]
        xk1 = 'XTO%d' % xi
        dma('sp', (x1, xv[row * 128:(row + 1) * 128, :]), 'xo%d' % xi, [], [xk1])
        for nh in range(2):
            for kc in range(8):
                mm(bank(4 + nh), mixT[:, kc, tc0:tc0 + 128], WO[:, kc, nh * 512:(nh + 1) * 512], kc == 0, kc == 7,
                   ['mixT', 'WBa'], [bk(4 + nh)])
        for nh in range(2):
            tt('dve', x1[:, nh * 512:(nh + 1) * 512], bank(4 + nh), x1[:, nh * 512:(nh + 1) * 512], ALU.add,
               [bk(4 + nh), xk1], [xk1])
        ft, fk = FTO[xi], 'FTO%d' % xi
        norm_T(x1, xk1, G_XA, ft[:], fk)
        return dict(bo=bo, tc0=tc0, xi=xi, x1=x1, xk1=xk1, ft=ft, fk=fk)

    def st1(c):
        b0 = 2 * SL['s']
        proj2(c['ft'], c['fk'], WQ, ['WBa'], 0, b0)
        hb, hbk = next_hb()
        head4_norm(b0, hb[:], hbk)
        c['xqT'], c['xqk'] = next_ft()
        transpose_gain(hb, hbk, 8, gcols[:, G_XQ:G_XQ + 8], c['xqT'][:], c['xqk'])

    def st2(c):
        b0 = 2 * SL['s']
        so = 128 * SL['s']
        sx = '_%d' % SL['s']
        xqT, xqk = c['xqT'], c['xqk']
        for h in range(4):
            for dc in range(2):
                mm(bank(b0 + h // 2)[:, (h % 2) * 256:(h % 2 + 1) * 256], xqT[:, 2 * h + dc, :], xkT[:, 2 * h + dc, :],
                   dc == 0, dc == 1, [xqk, 'xkT'], [bk(b0 + h // 2)])
        for nh in range(2):
            reduce_(st[:, so + 60 + 2 * nh:so + 62 + 2 * nh], bank(b0 + nh).rearrange("p (h m) -> p h m", h=2), ALU.max,
                    [bk(b0 + nh)], ['st_mx' + sx])
        ts('dve', st[:, so + 64:so + 68], st[:, so + 60:so + 64], -1.0 / 16, ALU.mult, ['st_mx' + sx], ['st_nm' + sx])
        pb, pk = next_hb()
        for h in range(4):
            act(pb[:, h * 256:(h + 1) * 256], bank(b0 + h // 2)[:, (h % 2) * 256:(h % 2 + 1) * 256], AF.Exp,
                [bk(b0 + h // 2), 'st_nm' + sx], [pk, 'st_sm' + sx], bias=st[:, so + 64 + h:so + 65 + h], scale=1.0 / 16,
                accum_out=st[:, so + 68 + h:so + 69 + h])
        S.add('dve', lambda e: e.reciprocal(out=st[:, so + 72:so + 76], in_=st[:, so + 68:so + 72]), ['st_sm' + sx], ['st_rs4' + sx])
        c['pT'], c['pTk'] = next_ft()
        transpose_gain(pb, pk, 8, None, c['pT'][:], c['pTk'])

    def st3(c):
        b0 = 2 * SL['s']
        so = 128 * SL['s']
        sx = '_%d' % SL['s']
        pT, pTk = c['pT'], c['pTk']
        for h in range(4):
            for mc in range(2):
                mm(bank(b0 + h // 2)[:, (h % 2) * 256:(h % 2 + 1) * 256], pT[:, 2 * h + mc, :],
                   xvb[:, mc, h * 256:(h + 1) * 256], mc == 0, mc == 1, [pTk, 'xvb'], [bk(b0 + h // 2)])
        xob, xok = next_hb()
        for h in range(4):
            ts('dve', xob[:, h * 256:(h + 1) * 256], bank(b0 + h // 2)[:, (h % 2) * 256:(h % 2 + 1) * 256],
               st[:, so + 72 + h:so + 73 + h], ALU.mult, [bk(b0 + h // 2), 'st_rs4' + sx], [xok])
        c['xoT'], c['xoTk'] = next_ft()
        transpose_gain(xob, xok, 8, None, c['xoT'][:], c['xoTk'])

    def st4(c):
        b0 = 2 * SL['s']
        x1, xk1, tc0, bo = c['x1'], c['xk1'], c['tc0'], c['bo']
        proj2(c['xoT'], c['xoTk'], WX, ['WBc'], 0, b0)
        for nh in range(2):
            tt('dve', x1[:, nh * 512:(nh + 1) * 512], bank(b0 + nh), x1[:, nh * 512:(nh + 1) * 512], ALU.add,
               [bk(b0 + nh), xk1], [xk1])
        norm_T(x1, xk1, G_MLP, hmT[:, :, tc0:tc0 + 128], 'hmT')
        dma('sp', (outd[bo * 128:(bo + 1) * 128, :], x1), 'x2o%d' % c['xi'], [xk1], ['x2d%d' % bo])

    ctxs = {0: frontO(0), 1: frontO(1)}
    for pbo in range(0, 16, 2):
        if pbo + 2 < 16:
            for s_ in range(2):
                SL['s'] = s_
                ctxs[pbo + 2 + s_] = frontO(pbo + 2 + s_)
        for stage in (st1, st2, st3, st4):
            for s_ in range(2):
                SL['s'] = s_
                stage(ctxs[pbo + s_])
    SL['s'] = 0
    barrier()
    esO1.close()

    YA = mixT[:].rearrange("p k t -> p (k t)").bitcast(F32)
    YB = WB[:, 0:16384].bitcast(F32)
    Yb = [YA[:, j * 1024:(j + 1) * 1024] for j in range(8)] + [YB[:, j * 1024:(j + 1) * 1024] for j in range(8)]
    Rr = [RrF[:, 0:512], RrF[:, 512:1024]]
    esO = ExitStack()
    WDt = sb("WDt", [128, 8192], BF16, esO)
    uTc = [KV[:, 16384 + i * 8192:16384 + (i + 1) * 8192].rearrange("p (f t) -> p f t", f=4) for i in range(2)]
    WUc = [WB[:, 16384 + i * 4096:16384 + (i + 1) * 4096].rearrange("p (k n) -> p k n", k=8) for i in range(2)]
    WDc = [WDt[:, i * 4096:(i + 1) * 4096].rearrange("p (f n) -> p f n", f=4) for i in range(2)]
    dma('sp', [(Yb[j], outd[j * 128:(j + 1) * 128, :]) for j in range(16)], 'yinit',
        ['x2d%d' % j for j in range(16)], ['Y%d' % j for j in range(16)])
    ubc = [0]
    ybc = [0]
    for c in range(8):
        wb = c % 2
        dma('pool', [(WUc[wb][:, 0:4, :], w_up_v[:, 0:4, c * 512:(c + 1) * 512]),
                     (WUc[wb][:, 4:8, :], w_up_v[:, 4:8, c * 512:(c + 1) * 512])], 'wu%d' % wb, [], ['WU%d' % wb, 'WBc'])
        dma('pool', [(WDc[wb][:, 0:2, :], w_down_v[:, 4 * c:4 * c + 2, :]),
                     (WDc[wb][:, 2:4, :], w_down_v[:, 4 * c + 2:4 * c + 4, :])], 'wd%d' % wb, [], ['WD%d' % wb])
        for j in range(4):
            for t4 in range(4):
                ub = ubc[0] % 4
                ubc[0] += 1
                for kc in range(8):
                    mm(bank(ub), WUc[wb][:, kc, j * 128:(j + 1) * 128], hmT[:, kc, t4 * 512:(t4 + 1) * 512], kc == 0, kc == 7,
                       ['WU%d' % wb, 'hmT'], [bk(ub)])
                act(Rr[ub % 2], bank(ub), AF.Relu, [bk(ub)], ['Rr%d' % (ub % 2)])
                act(uTc[wb][:, j, t4 * 512:(t4 + 1) * 512], Rr[ub % 2], AF.Square, ['Rr%d' % (ub % 2)],
                    ['uT%d' % wb] + (['XTO2', 'XTO3', 'SCRx', 'SCR2x'] if c < 2 and j == 0 and t4 == 0 else []))
        for blk in range(16):
            for nh in range(2):
                yb = 4 + ybc[0] % 4
                ybc[0] += 1
                for j in range(4):
                    mm(bank(yb), uTc[wb][:, j, blk * 128:(blk + 1) * 128], WDc[wb][:, j, nh * 512:(nh + 1) * 512],
                       j == 0, j == 3, ['uT%d' % wb, 'WD%d' % wb], [bk(yb)])
                tt('dve', Yb[blk][:, nh * 512:(nh + 1) * 512], bank(yb), Yb[blk][:, nh * 512:(nh + 1) * 512], ALU.add,
                   [bk(yb), 'Y%d' % blk], ['Y%d' % blk])
    for j in range(16):
        dma('sp', (outd[j * 128:(j + 1) * 128, :], Yb[j]), 'out%d' % (j % 4), ['Y%d' % j], ['outd%d' % (j % 4)])
    S.add('sp', lambda e: None, ['outd%d' % tb for tb in range(4)], [])
    esO.close()
    return nc, es, S


def _host_consts():
    bf = ml_dtypes.bfloat16
    ident = np.eye(128, dtype=np.float32).astype(bf)
    j = np.arange(128)[:, None]
    s = np.arange(128)[None, :]
    negU = np.where(j >= s, -1.0, 0.0).astype(np.float32).astype(bf)
    negO = np.full((128, 128), -1.0, np.float32).astype(bf)
    cmask = np.where(j < s, 0.0, NEG).astype(np.float32).astype(bf)
    h = np.arange(4, dtype=np.float32)
    log_gamma = np.log1p(-(2.0 ** (-5.0 - h))).astype(np.float32)
    idx = np.arange(128, dtype=np.float32)
    scale = np.float32(128.0 ** -0.5)
    rel = idx[None, :] - idx[:, None]
    dec = np.zeros((128, 4, 128), np.float32)
    for hh in range(4):
        dec[:, hh, :] = np.where(rel >= 0, np.exp(log_gamma[hh] * np.where(rel >= 0, rel, 0.0)), 0.0) * scale
    qd = np.zeros((128, 4, 128), np.float32)
    for hh in range(4):
        qd[:, hh, :] = np.exp(log_gamma[hh] * (idx + 1.0))[None, :]
    kd = np.zeros((128, 4, 128), np.float32)
    for hh in range(4):
        kd[:, hh, :] = (np.exp(log_gamma[hh] * (127.0 - idx)) * scale)[:, None]
    invf = (10000.0 ** (-np.arange(0, 128, 2, dtype=np.float32) / 128)).astype(np.float32)
    invf = np.broadcast_to(invf[None, :], (128, 64)).copy()
    return dict(ident=ident, negU=negU, negO=negO, cmask=cmask,
                decayT=dec.reshape(128, 512), qdecT=qd.reshape(128, 512), kdec=kd.reshape(128, 512), invf=invf)


def kernel(x, mem, positions, g_mix, w_in, ret_gn_g, sb_q_g, sb_k_g, w_out, g_xattn, g_mem,
           w_xq, w_xkv, xq_g, xk_g, w_xo, g_mlp, w_up, w_down):
    x = np.asarray(x, np.float32)
    mem = np.asarray(mem, np.float32)
    positions = np.asarray(positions, np.int32)
    consts = _host_consts()

    def col(v):
        v = np.asarray(v, np.float32).reshape(-1, 128)
        return np.ascontiguousarray(v.T)

    gcols = np.concatenate([col(g_mix[0]), col(g_xattn[0]), col(g_mlp[0]), col(g_mem[0]),
                            col(xq_g[0]), col(xk_g[0]), col(ret_gn_g[0]), col(sb_q_g[0]), col(sb_k_g[0])],
                           axis=1).astype(np.float32)
    shared = dict(w_in=np.ascontiguousarray(w_in[0], np.float32), w_out=np.ascontiguousarray(w_out[0], np.float32),
                  w_xq=np.ascontiguousarray(w_xq[0], np.float32), w_xkv=np.ascontiguousarray(w_xkv[0], np.float32),
                  w_xo=np.ascontiguousarray(w_xo[0], np.float32), w_up=np.ascontiguousarray(w_up[0], np.float32),
                  w_down=np.ascontiguousarray(w_down[0], np.float32), gcols=gcols, **consts)
    in_maps = []
    for b in range(4):
        for g in range(2):
            if g == 0:
                xvv = np.concatenate([np.zeros((512, D), np.float32), x[b, :3584]], axis=0)
                pv = np.concatenate([np.zeros((512,), np.int32), positions[b, :3584]], axis=0)
                db = np.full((128, 1), NEG, np.float32)
            else:
                xvv = x[b]
                pv = positions[b]
                db = np.zeros((128, 1), np.float32)
            m = dict(shared)
            m.update(xv=np.ascontiguousarray(xvv), pos=np.ascontiguousarray(pv.reshape(NB, 128).T.astype(np.int32)),
                     dbias=db, mem=np.ascontiguousarray(mem[b]))
            in_maps.append(m)
    nc, es, S = build_program()
    with es:
        S.emit(nc, es)
    res = run_bass_kernel_spmd(nc, in_maps, core_ids=list(range(8)))
    out = np.zeros((4, SEQ, D), np.float32)
    for b in range(4):
        for g in range(2):
            o = res.results[2 * b + g]["out"]
            for ot in range(4):
                t = 2 * ot + g
                out[b, t * 512:(t + 1) * 512] = o[ot * 512:(ot + 1) * 512]
    return out
```
